# Optimizing a Trainium2 kernel written in Bass

```python
import jax
import jax.numpy as jnp
from jax import lax
import numpy as np

D_MODEL = 1024
BATCH = 16
SEQ = 2048
DEPTH = 1

GDN_HEADS = 8
GDN_DK = 128
GDN_DV = 128
GDN_CONV = 4
CHUNK = 64
SC_WIDTH = D_MODEL
SC_CONV = 3
D_FF_RAW = (8 * D_MODEL + 2) // 3
D_FF = (D_FF_RAW + 255) // 256 * 256
EPS = 1e-6
N_MOD = 6

QK_W = GDN_HEADS * GDN_DK
V_W = GDN_HEADS * GDN_DV
QKV_W = 2 * QK_W + V_W
Q_OFF = 0
K_OFF = Q_OFF + QK_W
V_OFF = K_OFF + QK_W
Z_OFF = V_OFF + V_W
A_OFF = Z_OFF + V_W
BETA_OFF = A_OFF + GDN_HEADS
SCB_OFF = BETA_OFF + GDN_HEADS
SCC_OFF = SCB_OFF + SC_WIDTH
SCX_OFF = SCC_OFF + SC_WIDTH
GA_OFF = SCX_OFF + SC_WIDTH
GB_OFF = GA_OFF + D_MODEL
IN_COLS = GB_OFF + D_MODEL

kernel_name = "cond_hybrid_gdn_shortconv_block"


def rms_norm(x, w):
    xf = x.astype(jnp.float32)
    y = xf * lax.rsqrt(jnp.mean(xf * xf, axis=-1, keepdims=True) + EPS)
    return (y * w.astype(jnp.float32)).astype(x.dtype)


def l2_normalize(x):
    xf = x.astype(jnp.float32)
    return (xf * lax.rsqrt(jnp.sum(xf * xf, axis=-1, keepdims=True) + EPS)).astype(x.dtype)


def modulate(h, shift, scale):
    return h * (1.0 + scale[:, None, :]) + shift[:, None, :]


def causal_depthwise_conv(x, w):
    width, ch = w.shape
    return lax.conv_general_dilated(
        x, w[:, None, :].astype(x.dtype), window_strides=(1,), padding=[(width - 1, 0)],
        dimension_numbers=("NWC", "WIO", "NWC"), feature_group_count=ch)


def gated_delta_rule_chunked(q, k, v, g, beta):
    f32 = jnp.float32
    out_dtype = v.dtype
    b, s, h, dk = q.shape
    dv = v.shape[-1]
    n = s // CHUNK

    def to_chunks(t):
        t = t.astype(f32).reshape((b, n, CHUNK, h) + t.shape[3:])
        return jnp.moveaxis(t, 3, 1)

    q, k, v, g, beta = map(to_chunks, (q, k, v, g, beta))
    q = q * (dk ** -0.5)
    g = jnp.cumsum(g, axis=-1)
    idx = jnp.arange(CHUNK)
    causal = idx[:, None] >= idx[None, :]
    strict = idx[:, None] > idx[None, :]
    decay = jnp.exp(jnp.where(causal, g[..., :, None] - g[..., None, :], -jnp.inf))

    k_beta = k * beta[..., None]
    v_beta = v * beta[..., None]
    lower = jnp.where(strict, jnp.einsum("bhncd,bhnsd->bhncs", k_beta, k) * decay, 0.0)
    eye = jnp.eye(CHUNK, dtype=f32)
    t_inv = lax.linalg.triangular_solve(eye + lower, jnp.broadcast_to(eye, lower.shape),
                                        left_side=True, lower=True, unit_diagonal=True)
    eg = jnp.exp(g)
    u = jnp.einsum("bhncs,bhnse->bhnce", t_inv, v_beta)
    w = jnp.einsum("bhncs,bhnsd->bhncd", t_inv, k_beta * eg[..., None])
    qk = jnp.where(causal, jnp.einsum("bhncd,bhnsd->bhncs", q, k) * decay, 0.0)
    q_g = q * eg[..., None]
    g_last = g[..., -1]
    k_dec = k * jnp.exp(g_last[..., None] - g)[..., None]

    def step(state, xs):
        q_c, qk_c, u_c, w_c, k_c, gl = xs
        v_new = u_c - jnp.einsum("bhcd,bhde->bhce", w_c, state)
        o = jnp.einsum("bhcd,bhde->bhce", q_c, state) + jnp.einsum("bhcs,bhse->bhce", qk_c, v_new)
        state = state * jnp.exp(gl)[..., None, None] + jnp.einsum("bhcd,bhce->bhde", k_c, v_new)
        return state, o

    xs = tuple(jnp.moveaxis(t, 2, 0) for t in (q_g, qk, u, w, k_dec, g_last))
    state0 = jnp.zeros((b, h, dk, dv), f32)
    _, o = lax.scan(step, state0, xs)
    o = jnp.transpose(o, (1, 0, 3, 2, 4)).reshape(b, s, h, dv)
    return o.astype(out_dtype)


def hybrid_mixer(h, w_in, gdn_conv_w, a_log, dt_bias, gdn_norm_w, w_gdn_proj, sc_conv_w, w_sc_out, w_o):
    b, s, _ = h.shape
    p = h @ w_in
    qkv = jax.nn.silu(causal_depthwise_conv(p[..., Q_OFF:Z_OFF], gdn_conv_w))
    q = l2_normalize(qkv[..., Q_OFF:K_OFF].reshape(b, s, GDN_HEADS, GDN_DK))
    k = l2_normalize(qkv[..., K_OFF:V_OFF].reshape(b, s, GDN_HEADS, GDN_DK))
    v = qkv[..., V_OFF:Z_OFF].reshape(b, s, GDN_HEADS, GDN_DV)
    z = p[..., Z_OFF:A_OFF].reshape(b, s, GDN_HEADS, GDN_DV)
    g = -jnp.exp(a_log) * jax.nn.softplus(p[..., A_OFF:BETA_OFF] + dt_bias)
    beta = jax.nn.sigmoid(p[..., BETA_OFF:SCB_OFF])
    o = gated_delta_rule_chunked(q, k, v, g, beta)
    o = rms_norm(o, gdn_norm_w) * jax.nn.silu(z)
    y_a = o.reshape(b, s, V_W) @ w_gdn_proj
    gb = p[..., SCB_OFF:SCC_OFF]
    gc = p[..., SCC_OFF:SCX_OFF]
    xin = p[..., SCX_OFF:GA_OFF]
    y_b = (gb * causal_depthwise_conv(gc * xin, sc_conv_w)) @ w_sc_out
    gate_a = jax.nn.sigmoid(p[..., GA_OFF:GB_OFF])
    gate_b = jax.nn.sigmoid(p[..., GB_OFF:IN_COLS])
    return (gate_a * y_a + gate_b * y_b) @ w_o


def swiglu(h, w_ffn_in, w_ffn_out):
    gu = h @ w_ffn_in
    return (jax.nn.silu(gu[..., :D_FF]) * gu[..., D_FF:]) @ w_ffn_out


def setup_inputs(seed: int = 0) -> dict:
    key = jax.random.key(seed)
    ks = jax.random.split(key, 22)
    nrm = jax.random.normal
    D = D_MODEL
    x = nrm(ks[0], (BATCH, SEQ, D), jnp.float32)
    c = nrm(ks[1], (BATCH, D), jnp.float32)
    w_ada = nrm(ks[2], (DEPTH, D, N_MOD * D), jnp.float32) * (0.5 * D ** -0.5)
    b_ada = 0.02 * nrm(ks[3], (DEPTH, N_MOD * D), jnp.float32)
    norm1_w = 1.0 + 0.02 * nrm(ks[4], (DEPTH, D), jnp.float32)
    w_in = nrm(ks[5], (DEPTH, D, IN_COLS), jnp.float32) * D ** -0.5
    gdn_conv_w = nrm(ks[6], (DEPTH, GDN_CONV, QKV_W), jnp.float32) * GDN_CONV ** -0.5
    gdn_a_log = jnp.log(jax.random.uniform(ks[7], (DEPTH, GDN_HEADS), jnp.float32, 1.0, 16.0))
    dt = jnp.exp(jax.random.uniform(ks[8], (DEPTH, GDN_HEADS), jnp.float32,
                                    float(np.log(1e-3)), float(np.log(1e-1))))
    gdn_dt_bias = jnp.log(jnp.expm1(dt))
    gdn_norm_w = 1.0 + 0.02 * nrm(ks[9], (DEPTH, GDN_DV), jnp.float32)
    w_gdn_proj = nrm(ks[10], (DEPTH, V_W, D), jnp.float32) * V_W ** -0.5
    sc_conv_w = nrm(ks[11], (DEPTH, SC_CONV, SC_WIDTH), jnp.float32) * SC_CONV ** -0.5
    w_sc_out = nrm(ks[12], (DEPTH, SC_WIDTH, D), jnp.float32) * SC_WIDTH ** -0.5
    w_o = nrm(ks[13], (DEPTH, D, D), jnp.float32) * D ** -0.5
    norm2_w = 1.0 + 0.02 * nrm(ks[14], (DEPTH, D), jnp.float32)
    w_ffn_in = nrm(ks[15], (DEPTH, D, 2 * D_FF), jnp.float32) * D ** -0.5
    w_ffn_out = nrm(ks[16], (DEPTH, D_FF, D), jnp.float32) * D_FF ** -0.5
    w_ada_f = nrm(ks[17], (D, 2 * D), jnp.float32) * (0.5 * D ** -0.5)
    b_ada_f = 0.02 * nrm(ks[18], (2 * D,), jnp.float32)
    normf_w = 1.0 + 0.02 * nrm(ks[19], (D,), jnp.float32)
    return {"x": x, "c": c, "w_ada": w_ada, "b_ada": b_ada, "norm1_w": norm1_w, "w_in": w_in,
            "gdn_conv_w": gdn_conv_w, "gdn_a_log": gdn_a_log, "gdn_dt_bias": gdn_dt_bias,
            "gdn_norm_w": gdn_norm_w, "w_gdn_proj": w_gdn_proj, "sc_conv_w": sc_conv_w,
            "w_sc_out": w_sc_out, "w_o": w_o, "norm2_w": norm2_w, "w_ffn_in": w_ffn_in,
            "w_ffn_out": w_ffn_out, "w_ada_f": w_ada_f, "b_ada_f": b_ada_f, "normf_w": normf_w}


def reference(x, c, w_ada, b_ada, norm1_w, w_in, gdn_conv_w, gdn_a_log, gdn_dt_bias, gdn_norm_w,
              w_gdn_proj, sc_conv_w, w_sc_out, w_o, norm2_w, w_ffn_in, w_ffn_out, w_ada_f, b_ada_f,
              normf_w):
    c_act = jax.nn.silu(c)
    for l in range(DEPTH):
        mod = c_act @ w_ada[l] + b_ada[l]
        sh1, sc1, g1, sh2, sc2, g2 = jnp.split(mod, N_MOD, axis=-1)
        h = modulate(rms_norm(x, norm1_w[l]), sh1, sc1)
        mix = hybrid_mixer(h, w_in[l], gdn_conv_w[l], gdn_a_log[l], gdn_dt_bias[l], gdn_norm_w[l],
                           w_gdn_proj[l], sc_conv_w[l], w_sc_out[l], w_o[l])
        x = x + g1[:, None, :] * mix
        h = modulate(rms_norm(x, norm2_w[l]), sh2, sc2)
        x = x + g2[:, None, :] * swiglu(h, w_ffn_in[l], w_ffn_out[l])
    shf, scf = jnp.split(c_act @ w_ada_f + b_ada_f, 2, axis=-1)
    return modulate(rms_norm(x, normf_w), shf, scf)
```

```python
import os
from contextlib import ExitStack
import numpy as np
import concourse.bass as bass
import concourse.mybir as mybir
from concourse.bass_utils import run_bass_kernel_spmd

F32 = mybir.dt.float32
BF16 = mybir.dt.bfloat16
ALU = mybir.AluOpType
AF = mybir.ActivationFunctionType

NCORES = 8
D = 1024
KC = 8
S_LEN = 2048
TB = 1024
NB = 4
NT = TB // 128
H = 8
DFF = 2816
NFF = DFF // 128
EPS = 1e-6
NW8 = 72 + 8 + 8 + 8 + 44
NSLOT8 = 4
NSLOT22 = 2
NAR = 24


class Op:
    __slots__ = ("eng", "fn", "deps", "is_dma", "sem", "val", "needs_inc", "dmakey", "seq", "cost", "pdeps", "prio", "tend")

    def __init__(self, eng, fn, is_dma, dmakey):
        self.eng = eng; self.fn = fn; self.deps = []; self.is_dma = is_dma
        self.sem = None; self.val = None; self.needs_inc = False; self.dmakey = dmakey
        self.seq = 0; self.cost = 0.3; self.pdeps = []; self.prio = 0.0; self.tend = 0.0


class Sched:
    ENGS = ("pe", "dve", "act", "pool", "sp")

    def __init__(self, nc):
        self.nc = nc
        self.ops = {e: [] for e in self.ENGS}
        self.last_w = {}
        self.readers = {}

    def mark(self):
        return {e: len(self.ops[e]) for e in self.ENGS}

    def reschedule(self, m0, m1):
        import heapq
        region = []
        for e in self.ENGS:
            region += self.ops[e][m0[e]:m1[e]]
        inreg = set(id(o) for o in region)
        LAT = 0.2
        region.sort(key=lambda o: o.seq)
        succ = {id(o): [] for o in region}
        npred = {}
        for o in region:
            ds = [d for d in (o.deps + o.pdeps) if id(d) in inreg]
            npred[id(o)] = len(ds)
            for d in ds:
                succ[id(d)].append(o)
        for o in reversed(region):
            best = 0.0
            for c in succ[id(o)]:
                best = max(best, c.prio + LAT)
            dl = 6.0 if o.is_dma else 0.0
            o.prio = o.cost + dl + best
        ready_t = {id(o): 0.0 for o in region}
        avail = {e: [] for e in self.ENGS}
        for o in region:
            if npred[id(o)] == 0:
                heapq.heappush(avail[o.eng], (-o.prio, o.seq, o))
        eng_t = {e: 0.0 for e in self.ENGS}
        order = {e: [] for e in self.ENGS}
        remaining = len(region)
        while remaining:
            best = None
            for e in self.ENGS:
                h = avail[e]
                if not h:
                    continue
                cands = heapq.nsmallest(6, h)
                tmin = min(max(eng_t[e], ready_t[id(c[2])]) for c in cands)
                pick = None
                for c in cands:
                    st = max(eng_t[e], ready_t[id(c[2])])
                    if st <= tmin + 0.3:
                        pick = c
                        break
                st = max(eng_t[e], ready_t[id(pick[2])])
                if best is None or st < best[0]:
                    best = (st, e, pick)
            st, e, pick = best
            avail[e].remove(pick)
            heapq.heapify(avail[e])
            o = pick[2]
            eng_t[e] = st + o.cost
            o.tend = st + o.cost + (6.0 if o.is_dma else 0.0)
            order[e].append(o)
            remaining -= 1
            for c in succ[id(o)]:
                ready_t[id(c)] = max(ready_t[id(c)], o.tend + (LAT if c.eng != e else 0.1))
                npred[id(c)] -= 1
                if npred[id(c)] == 0:
                    heapq.heappush(avail[c.eng], (-c.prio, c.seq, c))
        for e in self.ENGS:
            assert len(order[e]) == m1[e] - m0[e]
            self.ops[e][m0[e]:m1[e]] = order[e]
        return max(eng_t.values())

    def op(self, eng, fn, reads=(), writes=(), dma=False, dmakey=None, cost=None):
        o = Op(eng, fn, dma, dmakey)
        self.nseq = getattr(self, "nseq", 0) + 1
        o.seq = self.nseq
        if cost is not None:
            o.cost = cost
        elif dma:
            o.cost = 1.0 if eng == "pool" else 0.3
        deps = []
        for k in reads:
            w = self.last_w.get(k)
            if w is not None:
                deps.append(w)
            self.readers.setdefault(k, []).append(o)
        for k in writes:
            w = self.last_w.get(k)
            if w is not None:
                deps.append(w)
            deps.extend(self.readers.get(k, ()))
            self.last_w[k] = o
            self.readers[k] = []
        seen = set()
        for d in deps:
            if d is o or id(d) in seen:
                continue
            if d.eng == "pe" and eng == "pe" and not d.is_dma and not dma:
                if id(d) not in seen:
                    seen.add(id(d)); o.pdeps.append(d)
                continue
            seen.add(id(d))
            o.deps.append(d)
            d.needs_inc = True
        self.ops[eng].append(o)
        return o

    def emit(self, stack):
        nc = self.nc
        ROT = 12000
        lp = None
        for o in self.ops["pool"]:
            if o.is_dma:
                continue
            if lp is not None and all(d is not lp for d in o.deps):
                o.deps.append(lp)
                lp.needs_inc = True
            lp = o
        dma_sems = {}
        nsem = [0]

        def newsem(tag):
            nsem[0] += 1
            return stack.enter_context(nc.semaphore("s%d_%s" % (nsem[0], tag)))

        for e in self.ENGS:
            cur = None; cnt = 0
            for o in self.ops[e]:
                if o.is_dma:
                    key = o.dmakey
                    if key not in dma_sems:
                        dma_sems[key] = [newsem("d"), 0]
                    ent = dma_sems[key]
                    ent[1] += 16
                    if ent[1] > 30000:
                        ent[0] = newsem("d"); ent[1] = 16
                    o.sem = ent[0]; o.val = ent[1]
                elif o.needs_inc:
                    if cur is None or cnt >= ROT:
                        cur = newsem(e); cnt = 0
                    cnt += 1
                    o.sem = cur; o.val = cnt
        engmap = {"pe": "tensor", "dve": "vector", "act": "scalar", "pool": "gpsimd", "sp": "sync"}
        block = stack.enter_context(nc.Block())
        for e in self.ENGS:
            ops = self.ops[e]

            def body(eng, ops=ops):
                waited = {}
                for o in ops:
                    for d in o.deps:
                        sid = id(d.sem)
                        if waited.get(sid, 0) >= d.val:
                            continue
                        eng.wait_ge(d.sem, d.val)
                        waited[sid] = d.val
                    ins = o.fn(eng)
                    if o.is_dma:
                        ins.then_inc(o.sem, 16)
                    elif o.needs_inc:
                        ins.then_inc(o.sem, 1)
            getattr(block, engmap[e])(body)


def _masks():
    idx = np.arange(128)
    same = (idx[:, None] // 64) == (idx[None, :] // 64)
    m2 = (same & (idx[:, None] <= idx[None, :])).astype(np.float32)
    ms = (same & (idx[None, :] < idx[:, None])).astype(np.float32)
    m2t = np.ascontiguousarray(m2.T)
    ca = np.repeat((idx < 64).astype(np.float32)[:, None], 128, axis=1)
    cb = np.repeat((idx >= 64).astype(np.float32)[:, None], 128, axis=1)
    ident = np.eye(128, dtype=np.float32)
    ones = np.ones((128, 128), np.float32)
    cf = np.concatenate([ident, ones, m2, ms, m2t, ca, cb], axis=1)
    mls = []
    for sz in (1, 2, 4, 8, 16, 32):
        blk = (idx[:, None] // (2 * sz)) == (idx[None, :] // (2 * sz))
        mls.append((blk & ((idx[:, None] % (2 * sz)) >= sz) & ((idx[None, :] % (2 * sz)) < sz)).astype(np.float32))
    cbf = np.concatenate([ident, ones / 1024.0, ones, ones / 128.0] + mls + [np.ascontiguousarray(mls[0].T)], axis=1)
    return np.ascontiguousarray(cf), np.ascontiguousarray(cbf)


def build_nc(stage=99, nblocks=NB, dbg_cols=0):
    nc = bass.Bass("TRN2", target_bir_lowering=False)
    dr = lambda name, shape, dt=F32, kind="ExternalInput": nc.dram_tensor(name, shape, dt, kind=kind).ap()
    x_d = dr("x", [NB * TB, D])
    out_d = dr("out", [NB * TB, D], kind="ExternalOutput")
    cT_d = dr("cT", [128, KC * 2])
    wada_d = dr("wada", [48, 128, KC * 128])
    bada_d = dr("bada", [128, 48])
    wadaf_d = dr("wadaf", [16, 128, KC * 128])
    badaf_d = dr("badaf", [128, 16])
    nw_d = dr("nw", [128, 24])
    gcw_d = dr("gcw", [128, 24 * 4])
    scw_d = dr("scw", [128, 8 * 3])
    hp_d = dr("hp", [128, 16])
    gnw_d = dr("gnw", [128, 1])
    wab_d = dr("wab", [128, KC * 16])
    w8_d = dr("w8", [NW8, 128, KC * 128])
    w22_d = dr("w22", [8, 128, NFF * 128])
    cf_d = dr("cf", [128, 7 * 128])
    cbf_d = dr("cbf", [128, 11 * 128])
    dbg_d = dr("dbg", [128, max(dbg_cols, 1)], kind="ExternalOutput") if dbg_cols else None

    st = ExitStack()
    sb = lambda name, shape, dt=F32: st.enter_context(nc.sbuf_tensor(name, shape, dt))
    CF = sb("CF", [128, 7 * 128]); CB = sb("CBc", [128, 11 * 128], BF16)
    identF = CF[:, 0:128]; onesF = CF[:, 128:256]; M2 = CF[:, 256:384]; MS = CF[:, 384:512]
    M2T = CF[:, 512:640]; CAm = CF[:, 640:768]; CBm = CF[:, 768:896]
    identB = CB[:, 0:128]; onesD = CB[:, 128:256]; onesK = CB[:, 256:384]; onesV = CB[:, 384:512]
    MLm = [CB[:, (4 + i) * 128:(5 + i) * 128] for i in range(6)]; MU1 = CB[:, 10 * 128:11 * 128]
    cT = sb("cTs", [128, KC * 2]); cA = sb("cA", [128, KC * 2])
    bada = sb("badas", [128, 48]); badaf = sb("badafs", [128, 16])
    modT = sb("modT", [128, 48 * 2]); modF = sb("modF", [128, 16 * 2])
    nw = sb("nws", [128, 24]); gcw = sb("gcws", [128, 96]); scw = sb("scws", [128, 24])
    hp = sb("hps", [128, 16]); negA = sb("negA", [128, 8]); gnw = sb("gnws", [128, 1])
    wab = sb("wabs", [128, KC * 16], BF16)
    wm1 = sb("wm1", [128, 16]); sh1 = sb("sh1", [128, 16]); g1 = sb("g1", [128, 16])
    wm2 = sb("wm2", [128, 16]); sh2 = sb("sh2", [128, 16]); g2 = sb("g2", [128, 16])
    wmf = sb("wmf", [128, 16]); shf = sb("shf", [128, 16])
    xT = sb("xT", [128, KC * TB])
    hT = sb("hT", [128, KC * TB], BF16)
    AR = sb("AR", [128, NAR * TB], BF16)
    stage_x = [sb("stg%d" % i, [128, D]) for i in range(2)]
    wst = stage_x
    PCb = [sb("PCb%d" % i, [128, TB + 3], BF16) for i in range(2)]; DG = [sb("DG%d" % i, [128, 512], BF16) for i in range(2)]
    SIL = sb("SIL", [128, TB]); SGA = sb("SGA", [128, TB]); SQC = sb("SQC", [128, TB], BF16)
    RST = sb("RST", [128, TB])
    halo = sb("halo", [128, 32 * 3], BF16)
    kbeg = [sb("kbeg%d" % i, [128, NT * 128], BF16) for i in range(2)]; kdec = [sb("kdec%d" % i, [128, NT * 128], BF16) for i in range(2)]
    vb = [sb("vb%d" % i, [128, NT * 128], BF16) for i in range(2)]
    W8 = [sb("W8_%d" % i, [128, KC * 128], BF16) for i in range(NSLOT8)]
    W22 = [sb("W22_%d" % i, [128, NFF * 128], BF16) for i in range(NSLOT22)]
    ab = sb("ab", [128, NT * 16]); a1 = sb("a1", [128, NT * 8]); graw = sb("graw", [128, NT * 8])
    beta = sb("beta", [128, NT * 8]); EX = sb("EX", [128, NT * 32]); beg = sb("beg", [128, NT * 8])
    Sst = sb("Sst", [128, H * 128]); Sbf = sb("Sbf", [128, H * 128], BF16)
    GM = sb("GM", [128, 512]); SBm = sb("SBm", [128, 512], BF16); Dm = sb("Dm", [128, 512], BF16)
    DD = sb("DD", [128, 512], BF16)
    Lb = sb("Lb", [128, 512], BF16); Ub = sb("Ub", [128, 512], BF16); QKb = sb("QKb", [128, 512], BF16)
    QKT = [sb("QKT%d" % i, [128, 512], BF16) for i in range(2)]
    Pb = [sb("Pb%d" % i, [128, 512], BF16) for i in range(2)]
    Qb = [sb("Qb%d" % i, [128, 512], BF16) for i in range(2)]
    Xb = [sb("Xb%d" % i, [128, 512], BF16) for i in range(2)]
    EGB = sb("EGB", [128, 512], BF16); QG = [sb("QG%d" % i, [128, 512], BF16) for i in range(2)]
    Usb = [sb("Usb%d" % i, [128, 512], BF16) for i in range(2)]; NWT = [sb("NWT%d" % i, [128, 512], BF16) for i in range(2)]
    VN = sb("VN", [128, 128], BF16)
    OT = sb("OT", [128, TB])
    TMP = sb("TMP", [128, TB])
    PS = [st.enter_context(nc.psum_tensor("PS%d" % i, [128, 512], F32)) for i in range(8)]

    S = Sched(nc)

    def _n(ap):
        try:
            return int(ap.free_size())
        except Exception:
            return 512

    def ACT(out, in_, func, reads, writes, bias=None, scale=None):
        kw = {}
        if bias is not None:
            kw["bias"] = bias
        if scale is not None:
            kw["scale"] = scale
        return S.op("act", lambda e: e.activation(out=out, in_=in_, func=func, **kw), reads=reads, writes=writes, cost=0.22 + _n(out) / 1150.0)

    def _vc(eng, out):
        return (0.2 + _n(out) / 1000.0) if eng == "dve" else (0.65 + _n(out) / 600.0)

    def TT(eng, out, in0, in1, op, reads, writes):
        return S.op(eng, lambda e: e.tensor_tensor(out=out, in0=in0, in1=in1, op=op), reads=reads, writes=writes, cost=_vc(eng, out))

    def STT(eng, out, in0, scalar, in1, op0, op1, reads, writes):
        return S.op(eng, lambda e: e.scalar_tensor_tensor(out=out, in0=in0, scalar=scalar, in1=in1, op0=op0, op1=op1), reads=reads, writes=writes,
                    cost=_vc(eng, out))

    def TS(eng, out, in0, s1, op0, reads, writes):
        return S.op(eng, lambda e: e.tensor_scalar(out=out, in0=in0, scalar1=s1, scalar2=None, op0=op0), reads=reads, writes=writes, cost=_vc(eng, out))

    def CP(eng, out, in_, reads, writes):
        if eng == "act":
            return S.op("act", lambda e: e.activation(out=out, in_=in_, func=AF.Copy), reads=reads, writes=writes, cost=0.22 + _n(out) / 1150.0)
        return S.op(eng, lambda e: e.tensor_copy(out=out, in_=in_), reads=reads, writes=writes, cost=_vc(eng, out))

    def MMG(lst, reads, writes):
        lst = list(lst)
        cost = sum(0.06 + max(_n(r_), 32) / 2800.0 * (4 if l_.dtype == F32 else 1) for (o_, l_, r_, s0, s1) in lst)

        def f(e):
            ins = None
            for (o_, l_, r_, s0, s1) in lst:
                ins = e.matmul(o_, lhsT=l_, rhs=r_, start=s0, stop=s1)
            return ins
        return S.op("pe", f, reads=reads, writes=writes, cost=cost)

    def TRG(lst, reads, writes):
        lst = list(lst)

        def f(e):
            ins = None
            for (o_, i_, id_) in lst:
                ins = e.transpose(o_, i_, id_)
            return ins
        return S.op("pe", f, reads=reads, writes=writes, cost=0.11 * len(lst))

    psk = lambda i: ("ps", i)
    bank_ctr = [0]

    def bank():
        b = bank_ctr[0] % 8
        bank_ctr[0] += 1
        return b

    def ar(slot, lo=0, hi=TB):
        if slot == "SQC":
            return SQC[:, lo:hi]
        if slot == "TMPB":
            return TMP[:].bitcast(BF16)[:, lo:hi]
        return AR[:, slot * TB + lo: slot * TB + hi]

    def ark(slot):
        if slot == "TMPB":
            return ("TMP", 0)
        return ("ar", slot)

    def xTs(c, lo, hi):
        return xT[:, c * TB + lo: c * TB + hi]

    def hTs(c, lo, hi):
        return hT[:, c * TB + lo: c * TB + hi]

    dbg_off = [0]

    def dump(ap, key, ncols):
        if dbg_d is None:
            return
        o = dbg_off[0]
        dbg_off[0] += ncols
        keys = key if isinstance(key, list) else [key]
        S.op("sp", lambda e: e.dma_start(out=dbg_d[:, o:o + ncols], in_=ap), reads=keys, dma=True, dmakey="dbg%d" % o)

    def ld(dst, src, key, eng="sp"):
        S.op(eng, lambda e: e.dma_start(out=dst, in_=src), writes=[key], dma=True, dmakey=key)

    ld(CF[:], cf_d, "CF"); ld(CB[:], cbf_d, "CB", "pool"); ld(cT[:], cT_d, "cT"); ld(bada[:], bada_d, "bada")
    ld(badaf[:], badaf_d, "badaf"); ld(nw[:], nw_d, "nw"); ld(gcw[:], gcw_d, "gcw"); ld(scw[:], scw_d, "scw")
    ld(hp[:], hp_d, "hp"); ld(gnw[:], gnw_d, "gnw"); ld(wab[:], wab_d, "wab", "pool")
    EPSC = sb("EPSC", [128, 1]); ONEC = sb("ONEC", [128, 1])
    S.op("dve", lambda e: e.memset(EPSC[:], EPS), writes=["EPSC"])
    S.op("dve", lambda e: e.memset(ONEC[:], 1.0), writes=["ONEC"])

    ACT(cA[:], cT[:], AF.Silu, ["cT"], ["cA"])
    ACT(negA[:], hp[:, 0:8], AF.Exp, ["hp"], ["negA"])
    TS("dve", negA[:], negA[:], -1.0, ALU.mult, ["negA"], ["negA"])

    cAb = sb("cAb", [128, KC * 2], BF16)
    CP("dve", cAb[:], cA[:], ["cA"], ["cAb"])

    def adaln(w_dram, nchunk, outT, bias_t, bkey, tag):
        for j in range(nchunk):
            slot = j % NSLOT8
            src = w_dram[j]
            S.op("pool", (lambda dst, src: (lambda e: e.dma_start(out=dst, in_=src)))(W8[slot][:], src),
                 writes=[("w8", slot)], dma=True, dmakey=("w8", slot))
            b = bank()
            MMG([(PS[b][:, 0:2], W8[slot][:, kc * 128:(kc + 1) * 128], cAb[:, kc * 2:(kc + 1) * 2], kc == 0, kc == KC - 1) for kc in range(KC)],
                [("w8", slot), "cAb"], [psk(b)])
            TS("dve", outT[:, j * 2:(j + 1) * 2], PS[b][:, 0:2], bias_t[:, j:j + 1], ALU.add, [psk(b), bkey], [tag])

    adaln(wada_d, 48, modT, bada, "bada", "mod")
    adaln(wadaf_d, 16, modF, badaf, "badaf", "modf")
    modT3 = modT[:].rearrange("p (j b) -> p j b", b=2)
    modF3 = modF[:].rearrange("p (j b) -> p j b", b=2)

    def modcols(dst, src3, kind):
        for b_ in range(2):
            CP("dve", dst[:, b_ * 8:(b_ + 1) * 8], src3[:, kind * 8:(kind + 1) * 8, b_], ["mod", "modf"], [("mc", id(dst))])

    def wmcols(dst, src3, kind, nwoff):
        for b_ in range(2):
            STT("dve", dst[:, b_ * 8:(b_ + 1) * 8], src3[:, kind * 8:(kind + 1) * 8, b_], 1.0, nw[:, nwoff:nwoff + 8], ALU.add, ALU.mult,
                ["mod", "modf", "nw"], [("mc", id(dst))])

    modcols(sh1, modT3, 0); wmcols(wm1, modT3, 1, 0); modcols(g1, modT3, 2)
    modcols(sh2, modT3, 3); wmcols(wm2, modT3, 4, 8); modcols(g2, modT3, 5)
    modcols(shf, modF3, 0); wmcols(wmf, modF3, 1, 16)
    MCK = [("mc", id(t)) for t in (sh1, wm1, g1, sh2, wm2, g2, shf, wmf)]

    wl8 = []
    for h in range(H):
        wl8 += [h, 8 + h, 16 + h, 24 + h]
    for c in range(8):
        wl8 += [32 + c, 40 + c, 48 + c]
    for c in range(8):
        wl8 += [72 + c, 80 + c, 56 + c, 64 + c]
    for c in range(8):
        wl8 += [88 + c]
    for j in range(NFF):
        wl8 += [96 + j, 96 + NFF + j]
    assert len(wl8) == NW8
    w8_issued = [0]; w8_used = [0]
    tot8 = nblocks * NW8

    def w8_issue():
        i = w8_issued[0]
        if i >= tot8:
            return
        slot = i % NSLOT8
        dst = W8[slot][:]; src = w8_d[wl8[i % NW8]]
        S.op("pool", lambda e: e.dma_start(out=dst, in_=src), writes=[("w8", slot)], dma=True, dmakey=("w8", slot))
        w8_issued[0] += 1

    def w8_next(expect):
        i = w8_used[0]
        assert wl8[i % NW8] == expect, (i, wl8[i % NW8], expect)
        w8_used[0] += 1
        return i % NSLOT8

    w22_issued = [0]; w22_used = [0]
    tot22 = nblocks * 8

    def w22_issue():
        i = w22_issued[0]
        if i >= tot22:
            return
        slot = i % NSLOT22
        dst = W22[slot][:]; src = w22_d[i % 8]
        S.op("pool", lambda e: e.dma_start(out=dst, in_=src), writes=[("w22", slot)], dma=True, dmakey=("w22", slot))
        w22_issued[0] += 1

    for _ in range(NSLOT8 - 1):
        w8_issue()

    split_first = [0]

    def dense(widx, evac, bk=None):
        bk = bk or bank
        w8_issue()
        slot = w8_next(widx)
        for half in range(2):
            b = bk()
            if split_first[0] > 0:
                split_first[0] -= 1
                for kc in range(KC):
                    MMG([(PS[b][:], W8[slot][:, kc * 128:(kc + 1) * 128], hTs(kc, half * 512, half * 512 + 512), kc == 0, kc == KC - 1)],
                        [("w8", slot), ("hT", kc, half)], [psk(b)])
            else:
                MMG([(PS[b][:], W8[slot][:, kc * 128:(kc + 1) * 128], hTs(kc, half * 512, half * 512 + 512), kc == 0, kc == KC - 1) for kc in range(KC)],
                    [("w8", slot)] + [("hT", kc, half) for kc in range(KC)], [psk(b)])
            evac(half, b)

    def dense_ar(widx, slots, evac):
        w8_issue()
        slot = w8_next(widx)
        for half in range(2):
            b = bank()
            MMG([(PS[b][:], W8[slot][:, kc * 128:(kc + 1) * 128], ar(slots[kc], half * 512, half * 512 + 512), kc == 0, kc == KC - 1) for kc in range(KC)],
                [("w8", slot)] + [ark(s_) for s_ in slots], [psk(b)])
            evac(half, b)

    def rms_rstd(src_ap_fn, src_keys_fn, nch, ones_ap, half, sq_slots, bk=None):
        lo = half * 512
        sqap = lambda c: ar(sq_slots[c // 2], (c % 2) * 512, (c % 2) * 512 + 512)
        for c in range(nch):
            if nch == KC and c in (1, 4, 7):
                TT("pool", sqap(c), src_ap_fn(c, lo, lo + 512), src_ap_fn(c, lo, lo + 512), ALU.mult, src_keys_fn(c, half), [ark(sq_slots[c // 2])])
            else:
                ACT(sqap(c), src_ap_fn(c, lo, lo + 512), AF.Square, src_keys_fn(c, half), [ark(sq_slots[c // 2])])
        b = (bk or bank)()
        MMG([(PS[b][:], ones_ap, sqap(c), c == 0, c == nch - 1) for c in range(nch)],
            [ark(s_) for s_ in sq_slots[:(nch + 1) // 2]] + ["CB"], [psk(b)])
        ACT(RST[:, lo:lo + 512], PS[b][:], AF.Ln, [psk(b), "EPSC"], [("RST", half)], bias=EPSC[:, 0:1], scale=1.0)
        ACT(RST[:, lo:lo + 512], RST[:, lo:lo + 512], AF.Exp, [("RST", half)], [("RST", half)], scale=-0.5)

    def norm_mod(wm, sh, bsel, dst_fn, dst_keys_fn, sq_slots):
        for half in range(2):
            rms_rstd(xTs, lambda c, hf: [("xT", c, hf)], KC, onesD, half, sq_slots)
            lo = half * 512
            for c in range(KC):
                col = bsel * 8 + c
                tb_, tk_ = ((TMP, "TMP"), (SGA, "SGA"))[c % 2]
                STT("dve", tb_[:, lo:lo + 512], xTs(c, lo, lo + 512), wm[:, col:col + 1], RST[:, lo:lo + 512], ALU.mult, ALU.mult,
                    [("xT", c, half), ("RST", half)] + MCK, [(tk_, half)])
                if c in (3, 7):
                    TS("dve", dst_fn(c, lo, lo + 512), tb_[:, lo:lo + 512], sh[:, col:col + 1], ALU.add, [(tk_, half)] + MCK, dst_keys_fn(c, half))
                else:
                    ACT(dst_fn(c, lo, lo + 512), tb_[:, lo:lo + 512], AF.Identity, [(tk_, half)] + MCK, dst_keys_fn(c, half),
                        bias=sh[:, col:col + 1], scale=1.0)

    g3 = graw[:].rearrange("p (t k) -> p t k", k=8)
    b3 = beta[:].rearrange("p (t k) -> p t k", k=8)
    beg3 = beg[:].rearrange("p (t k) -> p t k", k=8)
    EX4 = EX[:].rearrange("p (t k h) -> p t k h", k=4, h=8)
    ab3 = ab[:].rearrange("p (t k) -> p t k", k=16)
    a13 = a1[:].rearrange("p (t k) -> p t k", k=8)
    r4 = lambda ap: ap.rearrange("p (a i) -> p a i", a=4)
    bc4 = lambda ap: ap.unsqueeze(1).to_broadcast([128, 4, 128])
    xT3 = xT[:].rearrange("p (c t) -> p c t", c=KC)
    SQs, SKs, SZs, STMP, STMP2 = (16, 17), (18, 19), (20, 21, 'SQC'), 22, 23
    SQCslot = 'SQC'
    SCs = (16, 17, 18, 19, 22, 23, 8, 9)
    MMs = (10, 11, 12, 13, 14, 15, 20, 21)

    mALL0 = S.mark()
    for blk in range(nblocks):
        seq = blk // 2
        first = (blk % 2 == 0)
        bsel = seq
        for t in range(NT):
            row0 = blk * TB + t * 128
            if t < 4:
                stg = hT[:, t * 2048:(t + 1) * 2048].bitcast(F32)
                skeys = [("hT", 2 * t + a_, b_) for a_ in range(2) for b_ in range(2)]
                dkey = ("xh", t)
            elif t == 4:
                stg = OT[:]; skeys = [("OT", 0), ("OT", 1)]; dkey = ("xh", t)
            elif t == 5:
                stg = SIL[:]; skeys = [("SILa", 0), ("SILa", 1)]; dkey = ("xh", t)
            else:
                sg = t % 2
                stg = stage_x[sg][:]
                skeys = [("stg", sg)]
                dkey = ("stg", sg)
            S.op("sp", (lambda dst, src: (lambda e: e.dma_start(out=dst, in_=src)))(stg, x_d[row0:row0 + 128, :]),
                 writes=skeys, dma=True, dmakey=dkey)
            for g in range(2):
                b = bank()
                TRG([(PS[b][:, q * 128:(q + 1) * 128], stg[:, (g * 4 + q) * 128:(g * 4 + q + 1) * 128], identF) for q in range(4)],
                    skeys + ["CF"], [psk(b)])
                CP("act" if g == 0 else "dve", xT3[:, g * 4:(g + 1) * 4, t * 128:(t + 1) * 128], r4(PS[b][:]),
                   [psk(b)], [("xT", g * 4 + q, t // 4) for q in range(4)])
        if stage == 1:
            dump(xT[:, 0:2048], [("xT", c_, h_) for c_ in range(2) for h_ in range(2)], 2048)
            break
        norm_mod(wm1, sh1, bsel, hTs, lambda c, hf: [("hT", c, hf)], [16, 17, 18, 19])
        if stage == 2:
            CP("dve", TMP[:], hT[:, 0:1024], [("hT", 0, 0), ("hT", 0, 1)], [("TMP", 0), ("TMP", 1)])
            dump(TMP[:], [("TMP", 0), ("TMP", 1)], 1024)
            break
        b = bank()
        for hf_ in range(2):
            MMG([(PS[b][:, t * 16:(t + 1) * 16], hTs(kc, t * 128, (t + 1) * 128), wab[:, kc * 16:(kc + 1) * 16], kc == 0, kc == KC - 1)
                 for t in range(hf_ * 4, hf_ * 4 + 4) for kc in range(KC)],
                [("hT", kc, hf_) for kc in range(KC)] + ["wab"], [psk(b)])
        CP("act", ab[:], PS[b][:, 0:NT * 16], [psk(b)], ["ab"])
        TT("dve", a13, ab3[:, :, 0:8], hp[:, 8:16].unsqueeze(1).to_broadcast([128, NT, 8]), ALU.add, ["ab", "hp"], ["a1"])
        ACT(a1[:], a1[:], AF.Exp, ["a1"], ["a1"])
        ACT(a1[:], a1[:], AF.Ln, ["a1", "ONEC"], ["a1"], bias=ONEC[:, 0:1], scale=1.0)
        TT("dve", g3, a13, negA[:].unsqueeze(1).to_broadcast([128, NT, 8]), ALU.mult, ["a1", "negA"], ["graw"])
        ACT(b3, ab3[:, :, 8:16], AF.Exp, ["ab"], ["beta"], scale=-1.0)
        TS("dve", beta[:], beta[:], 1.0, ALU.add, ["beta"], ["beta"])
        S.op("dve", lambda e: e.reciprocal(out=beta[:], in_=beta[:]), reads=["beta"], writes=["beta"])
        b = bank()
        MMG([(PS[b][:, t * 32 + k * 8: t * 32 + k * 8 + 8], lt, graw[:, t * 8:(t + 1) * 8], True, True)
             for t in range(NT) for k, lt in enumerate((M2, MS, CAm, CBm))], ["graw", "CF"], [psk(b)])
        ACT(EX[:], PS[b][:, 0:NT * 32], AF.Exp, [psk(b)], ["EX"])
        TT("dve", beg3, b3, EX4[:, :, 0, :], ALU.mult, ["beta", "EX"], ["beg"])
        if stage == 3:
            CP("dve", TMP[:, 0:64], graw[:], ["graw"], [("TMP", 0)])
            CP("dve", TMP[:, 64:128], beta[:], ["beta", ("TMP", 0)], [("TMP", 0)])
            CP("dve", TMP[:, 128:384], EX[:], ["EX", ("TMP", 0)], [("TMP", 0)])
            dump(TMP[:, 0:384], ("TMP", 0), 384)
            break
        if first:
            S.op("pool", lambda e: e.memset(Sst[:], 0.0), writes=[("S", h) for h in range(H)])
            S.op("pool", lambda e: e.memset(Sbf[:], 0.0), writes=[("Sbf", h) for h in range(H)])
            S.op("pool", lambda e: e.memset(halo[:], 0.0), writes=[("halo", i) for i in range(32)])

        KBt = [(kbeg[i][:], kdec[i][:], vb[i][:], ("kbeg", i), ("kdec", i), ("vb", i)) for i in range(2)] + \
              [(ar(8), ar(9), ar(10), ark(8), ark(9), ark(10))]
        RSt = [(QG[i][:], QKT[i][:], NWT[i][:], Usb[i][:], ("QG", i), ("QKT", i), ("NWT", i), ("Usb", i)) for i in range(2)] + \
              [(ar(s_, 0, 512), ar(s_, 512, 1024), ar(s_ + 1, 0, 512), ar(s_ + 1, 512, 1024), ark(s_), ark(s_), ark(s_ + 1), ark(s_ + 1)) for s_ in (11, 13)]

        def _tiles(ap_bf, n):
            return [ap_bf[:, i * 512:(i + 1) * 512] for i in range(n)]
        t2 = _tiles(W22[0][:], 5) + _tiles(W22[1][:], 5) + _tiles(stage_x[0][:].bitcast(BF16)[:, 1024:2048], 2) + _tiles(stage_x[1][:].bitcast(BF16), 4)
        BS = [dict(GM=GM[:], SBm=SBm[:], Dm=Dm[:], DD=DD[:], Lb=Lb[:], Ub=Ub[:], QKb=QKb[:], P0=Pb[0][:], P1=Pb[1][:], Q0=Qb[0][:], Q1=Qb[1][:],
                   X0=Xb[0][:], X1=Xb[1][:], EGB=EGB[:], LOA=ar(15, 0, 512), LOB=ar(15, 512, 1024)),
              dict(GM=stage_x[0][:, 0:512], SBm=t2[0], Dm=t2[1], DD=t2[2], Lb=t2[3], Ub=t2[4], QKb=t2[5], P0=t2[6], P1=t2[7], Q0=t2[8], Q1=t2[9],
                   X0=t2[10], X1=t2[11], EGB=t2[12], LOA=t2[13], LOB=t2[14])]
        BK = [{n: n for n in BS[0]}, {n: ("B2", n) for n in BS[1]}]
        BK[0]["LOA"] = ark(15); BK[0]["LOB"] = ark(15)
        coarse = [("w22", 0), ("w22", 1), ("stg", 0), ("stg", 1)]
        S.op("pool", lambda e: e.nop(), writes=coarse + list(BK[1].values()))
        mD0 = S.mark()

        bkA = [0]; bkB = [0]

        def bankA():
            bkA[0] += 1
            return (0, 1, 2)[bkA[0] % 3]

        def bankB():
            bkB[0] += 1
            return (3, 4)[bkB[0] % 2]

        def sig_from_psum(b, half):
            sl = slice(half * 512, half * 512 + 512)
            ACT(SGA[:, sl], PS[b][:], AF.Exp, [psk(b)], [("SGA", half)], scale=-1.0)
            ACT(SGA[:, sl], SGA[:, sl], AF.Ln, [("SGA", half), "ONEC"], [("SGA", half)], bias=ONEC[:, 0:1], scale=1.0)
            ACT(SGA[:, sl], SGA[:, sl], AF.Exp, [("SGA", half)], [("SGA", half)], scale=-1.0)

        pcc = [0]

        def conv_pe(width, hidx, wt, wcol0, fill, consume, bk):
            i = pcc[0] % 2
            pcc[0] += 1
            pc, dg = PCb[i], DG[i]
            off = 3 - (width - 1)
            for j in range(width):
                ACT(dg[:, j * 128:(j + 1) * 128], identB, AF.Identity, ["CB", "gcw", "scw"], [("DG", i)], scale=wt[:, wcol0 + j:wcol0 + j + 1])
            CP("pool", pc[:, 0:3], halo[:, hidx * 3:hidx * 3 + 3], [("halo", hidx)], [("PCb", i, "h")])
            yield from fill(pc, i)
            CP("pool", halo[:, hidx * 3:hidx * 3 + 3], pc[:, TB:TB + 3], [("PCb", i, 1)], [("halo", hidx)])
            for half in range(2):
                b = bk()
                MMG([(PS[b][:], dg[:, j * 128:(j + 1) * 128], pc[:, off + j + half * 512: off + j + half * 512 + 512], j == 0, j == width - 1)
                     for j in range(width)], [("DG", i), ("PCb", i, 0)] + ([("PCb", i, "h")] if half == 0 else [("PCb", i, 1)]), [psk(b)])
                consume(half, b)
                yield

        def gen_A(h):
            p = h % 2
            kb_, kd_, vb_, kbk, kdk, vbk = KBt[h % 3]
            SZ_ = SZs[h % 3]
            for role in range(3):
                widx = role * 8 + h

                def fill(pc, i, widx=widx):
                    w8_issue()
                    slot = w8_next(widx)
                    for half in range(2):
                        b = bankA()
                        if h == 0 and widx == 0:
                            for kc in range(KC):
                                MMG([(PS[b][:], W8[slot][:, kc * 128:(kc + 1) * 128], hTs(kc, half * 512, half * 512 + 512), kc == 0, kc == KC - 1)],
                                    [("w8", slot), ("hT", kc, half)], [psk(b)])
                        else:
                            MMG([(PS[b][:], W8[slot][:, kc * 128:(kc + 1) * 128], hTs(kc, half * 512, half * 512 + 512), kc == 0, kc == KC - 1) for kc in range(KC)],
                                [("w8", slot)] + [("hT", kc, half) for kc in range(KC)], [psk(b)])
                        CP("act", pc[:, 3 + half * 512: 3 + half * 512 + 512], PS[b][:], [psk(b)], [("PCb", i, half)])
                        yield

                def consume(half, b, role=role):
                    sl = slice(half * 512, half * 512 + 512)
                    sig_from_psum(b, half)
                    if role < 2:
                        TT("dve", SIL[:, sl], PS[b][:], SGA[:, sl], ALU.mult, [psk(b), ("SGA", half)], [("SILa", half)])
                    else:
                        TT("dve", ar(STMP, half * 512, half * 512 + 512), PS[b][:], SGA[:, sl], ALU.mult, [psk(b), ("SGA", half)], [ark(STMP)])
                yield from conv_pe(4, widx, gcw, widx * 4, fill, consume, bankA)
                if role < 2:
                    dst = (SQs, SKs)[role][p]
                    scale = float(128 ** -0.5) if role == 0 else 1.0
                    for half in range(2):
                        lo = half * 512
                        sl = slice(lo, lo + 512)
                        TT("pool", ar(STMP2, lo, lo + 512), SIL[:, sl], SIL[:, sl], ALU.mult, [("SILa", half)], [ark(STMP2)])
                        b = bankA()
                        MMG([(PS[b][:], onesK, ar(STMP2, lo, lo + 512), True, True)], [ark(STMP2), "CB"], [psk(b)])
                        ACT(SGA[:, sl], PS[b][:], AF.Ln, [psk(b), "EPSC"], [("SGA", half)], bias=EPSC[:, 0:1], scale=1.0)
                        ACT(SGA[:, sl], SGA[:, sl], AF.Exp, [("SGA", half)], [("SGA", half)], scale=-0.5)
                        STT("dve", ar(dst, lo, lo + 512), SIL[:, sl], scale, SGA[:, sl], ALU.mult, ALU.mult,
                            [("SILa", half), ("SGA", half)], [ark(dst)])
                        yield
            w8_issue()
            slot = w8_next(24 + h)
            for half in range(2):
                b = bankA()
                MMG([(PS[b][:], W8[slot][:, kc * 128:(kc + 1) * 128], hTs(kc, half * 512, half * 512 + 512), kc == 0, kc == KC - 1) for kc in range(KC)],
                    [("w8", slot)] + [("hT", kc, half) for kc in range(KC)], [psk(b)])
                sig_from_psum(b, half)
                TT("dve", ar(SZ_, half * 512, half * 512 + 512), PS[b][:], SGA[:, half * 512: half * 512 + 512], ALU.mult,
                   [psk(b), ("SGA", half)], [ark(SZ_)])
                yield
            for src, which in ((STMP, "v"), (SKs[p], "k")):
                b = bankA()
                psb = PS[b][:].bitcast(BF16)
                TRG([(psb[:, t * 128:(t + 1) * 128], ar(src, t * 128, (t + 1) * 128), identB) for t in range(NT)], [ark(src), "CB"], [psk(b)])
                ps3 = psb.rearrange("p (t d) -> p t d", d=128)
                if which == "v":
                    TT("dve", vb_.rearrange("p (t d) -> p t d", d=128), ps3, b3[:, :, h:h + 1].to_broadcast([128, NT, 128]), ALU.mult,
                       [psk(b), "beta"], [vbk])
                else:
                    TT("dve", kb_.rearrange("p (t d) -> p t d", d=128), ps3, beg3[:, :, h:h + 1].to_broadcast([128, NT, 128]), ALU.mult,
                       [psk(b), "beg"], [kbk])
                    TT("dve", kd_.rearrange("p (t d) -> p t d", d=128), ps3, EX4[:, :, 1, h:h + 1].to_broadcast([128, NT, 128]), ALU.mult,
                       [psk(b), "EX"], [kdk])
                yield

        def gen_B(h, job):
            p = h % 2
            B_, K_ = BS[job], BK[job]
            kb_, kd_, vb_, kbk, kdk, vbk = KBt[h % 3]
            QG_, QKT_, NWT_, Usb_, QGk, QKTk, NWTk, Usbk = RSt[p * 2 + job]
            SQ, SK = SQs[p], SKs[p]
            p0 = job * 4
            tk = lambda pi: slice((p0 + pi) * 128, (p0 + pi + 1) * 128)
            pl = lambda pi: slice(pi * 128, (pi + 1) * 128)
            GM_, SBm_, Dm_, DD_, Lb_, Ub_, QKb_, EGB_ = B_["GM"], B_["SBm"], B_["Dm"], B_["DD"], B_["Lb"], B_["Ub"], B_["QKb"], B_["EGB"]
            TT("pool", r4(GM_), bc4(M2), g3[:, p0:p0 + 4, h:h + 1].to_broadcast([128, 4, 128]), ALU.mult, ["graw", "CF"], [K_["GM"]])
            TT("pool", r4(EGB_), bc4(M2), g3[:, p0:p0 + 4, h:h + 1].to_broadcast([128, 4, 128]), ALU.mult, ["graw", "CF"], [K_["EGB"]])
            TT("pool", r4(SBm_), bc4(MS), b3[:, p0:p0 + 4, h:h + 1].to_broadcast([128, 4, 128]), ALU.mult, ["beta", "CF"], [K_["SBm"]])
            bE, bG = bankB(), bankB()
            MMG([(PS[bE][:, pl(pi)], GM_[:, pl(pi)], MS, True, True) for pi in range(4)] +
                [(PS[bG][:, pl(pi)], onesK, EGB_[:, pl(pi)], True, True) for pi in range(4)],
                [K_["GM"], K_["EGB"], "CF", "CB"], [psk(bE), psk(bG)])
            ACT(Dm_, PS[bE][:], AF.Exp, [psk(bE)], [K_["Dm"]])
            ACT(EGB_, PS[bG][:], AF.Exp, [psk(bG)], [K_["EGB"]])
            yield
            TT("pool", SBm_, Dm_, SBm_, ALU.mult, [K_["Dm"], K_["SBm"]], [K_["SBm"]])
            TT("pool", r4(DD_), r4(Dm_), bc4(M2T), ALU.mult, [K_["Dm"], "CF"], [K_["DD"]])
            bA, bQK = bankB(), bankB()
            MMG([(PS[bA][:, pl(pi)], ar(SK)[:, tk(pi)], ar(SK)[:, tk(pi)], True, True) for pi in range(4)] +
                [(PS[bQK][:, pl(pi)], ar(SQ)[:, tk(pi)], ar(SK)[:, tk(pi)], True, True) for pi in range(4)],
                [ark(SK), ark(SQ)], [psk(bA), psk(bQK)])
            TT("dve", Lb_, PS[bA][:], SBm_, ALU.mult, [psk(bA), K_["SBm"]], [K_["Lb"]])
            TT("dve", QKb_, PS[bQK][:], DD_, ALU.mult, [psk(bQK), K_["DD"]], [K_["QKb"]])
            TT("pool", QG_, ar(SQ, p0 * 128, p0 * 128 + 512), EGB_, ALU.mult, [ark(SQ), K_["EGB"]], [QGk])
            yield
            bU, bT = bankB(), bankB()
            pu = PS[bU][:].bitcast(BF16); pt = PS[bT][:].bitcast(BF16)
            TRG([(pu[:, pl(pi)], Lb_[:, pl(pi)], identB) for pi in range(4)] + [(pt[:, pl(pi)], QKb_[:, pl(pi)], identB) for pi in range(4)],
                [K_["Lb"], K_["QKb"], "CB"], [psk(bU), psk(bT)])
            CP("act", Ub_, pu[:, 0:512], [psk(bU)], [K_["Ub"]])
            CP("act", QKT_, pt[:, 0:512], [psk(bT)], [QKTk])
            TT("pool", r4(B_["P0"]), r4(Lb_), bc4(MLm[0]), ALU.mult, [K_["Lb"], "CB"], [K_["P0"]])
            TT("dve", r4(B_["Q0"]), bc4(identB), r4(B_["P0"]), ALU.subtract, [K_["P0"], "CB"], [K_["Q0"]])
            yield
            TT("pool", r4(B_["P1"]), r4(Ub_), bc4(MU1), ALU.mult, [K_["Ub"], "CB"], [K_["P1"]])
            TT("dve", r4(B_["X0"]), bc4(identB), r4(B_["P1"]), ALU.subtract, [K_["P1"], "CB"], [K_["X0"]])
            LOs = [("LOA", "dve"), ("LOB", "pool"), ("P1", "pool"), ("QKb", "pool"), ("Dm", "pool")]
            for li in range(5):
                nm, eng_ = LOs[li]
                TT(eng_, r4(B_[nm]), r4(Lb_), bc4(MLm[li + 1]), ALU.mult, [K_["Lb"], "CB"], [K_[nm]])
            yield
            Tdc, Tdk, Rc, Rk = B_["Q0"], K_["Q0"], B_["X0"], K_["X0"]
            M1, M1k = B_["P0"], K_["P0"]
            for li in range(5):
                LO, LOk = B_[LOs[li][0]], K_[LOs[li][0]]
                xn = "X%d" % ((li + 1) % 2); qn = "Q%d" % ((li + 1) % 2)
                Rn, Rnk, Tdn, Tdnk = B_[xn], K_[xn], B_[qn], K_[qn]
                bP = bankB()
                MMG([(PS[bP][:, pl(pi)], LO[:, pl(pi)], Rc[:, pl(pi)], True, True) for pi in range(4)], [LOk, Rk], [psk(bP)])
                CP("act", M1, PS[bP][:], [psk(bP)], [M1k])
                yield
                bX = bankB()
                MMG([(PS[bX][:, pl(pi)], Tdc[:, pl(pi)], M1[:, pl(pi)], True, True) for pi in range(4)], [Tdk, M1k], [psk(bX)])
                TT("dve", Rn, Rc, PS[bX][:], ALU.subtract, [psk(bX), Rk], [Rnk])
                yield
                if li < 4:
                    bQ = bankB()
                    pq = PS[bQ][:].bitcast(BF16)
                    TRG([(pq[:, pl(pi)], Rn[:, pl(pi)], identB) for pi in range(4)], [Rnk, "CB"], [psk(bQ)])
                    CP("dve", Tdn, pq[:, 0:512], [psk(bQ)], [Tdnk])
                    yield
                Tdc, Tdk, Rc, Rk = Tdn, Tdnk, Rn, Rnk
            TTm, TTk = Rc, Rk
            bUu, bW = bankB(), bankB()
            MMG([(PS[bUu][:, pl(pi)], TTm[:, pl(pi)], vb_[:, tk(pi)], True, True) for pi in range(4)] +
                [(PS[bW][:, pl(pi)], kb_[:, tk(pi)], TTm[:, pl(pi)], True, True) for pi in range(4)],
                [TTk, vbk, kbk], [psk(bUu), psk(bW)])
            CP("act", Usb_, PS[bUu][:], [psk(bUu)], [Usbk])
            ACT(NWT_, PS[bW][:], AF.Identity, [psk(bW)], [NWTk], scale=-1.0)
            yield

        def gen_C(h, job):
            p = h % 2
            kb_, kd_, vb_, kbk, kdk, vbk = KBt[h % 3]
            QG_, QKT_, NWT_, Usb_, QGk, QKTk, NWTk, Usbk = RSt[p * 2 + job]
            jp = job
            p0 = job * 4
            hs = slice(h * 128, (h + 1) * 128)
            pl = lambda pi: slice(pi * 128, (pi + 1) * 128)
            bO = 6
            for pi in range(4):
                t = p0 + pi
                for ci in range(2):
                    r = slice(ci * 64, ci * 64 + 64)
                    cs = slice(pi * 128 + ci * 64, pi * 128 + ci * 64 + 64)
                    MMG([(PS[5][r, 0:128], NWT_[:, cs], Sbf[:, hs], True, True)], [NWTk, ("Sbf", h)], [psk(5)])
                    TT("dve", VN[r, :], PS[5][r, 0:128], Usb_[r, pl(pi)], ALU.add, [psk(5), Usbk], ["VN"])
                    yield
                    MMG([(PS[bO][:, cs], Sbf[:, hs], QG_[:, cs], True, False), (PS[bO][:, cs], VN[r, :], QKT_[r, cs], False, True)],
                        [("Sbf", h), QGk, "VN", QKTk], [psk(bO)])
                    MMG([(PS[7][:, 128:256], kd_[r, t * 128:(t + 1) * 128], VN[r, :], True, True)], [kdk, "VN"], [psk(7)])
                    STT("dve", Sbf[:, hs], Sst[:, hs], EX4[:, t, 2 + ci, h:h + 1], PS[7][:, 128:256], ALU.mult, ALU.add,
                        [psk(7), ("S", h), "EX"], [("Sbf", h)])
                    STT("dve", Sst[:, hs], Sst[:, hs], EX4[:, t, 2 + ci, h:h + 1], PS[7][:, 128:256], ALU.mult, ALU.add,
                        [psk(7), ("S", h), "EX"], [("S", h)])
                    yield
            CP("act", OT[:, p0 * 128:p0 * 128 + 512], PS[bO][:], [psk(bO)], [("OT", job)])
            yield

        def gen_onorm(h):
            p = h % 2
            for half in range(2):
                rms_rstd(lambda c, lo, hi: OT[:, lo:hi], lambda c, hf: [("OT", hf)], 1, onesV, half, ["TMPB"], bk=lambda: 6)
                yield
            STT("dve", TMP[:], OT[:], gnw[:, 0:1], RST[:], ALU.mult, ALU.mult,
                [("OT", 0), ("OT", 1), ("RST", 0), ("RST", 1), "gnw"], [("TMP", 0), ("TMP", 1)])
            TT("dve", ar(h), TMP[:], ar(SZs[h % 3]), ALU.mult, [("TMP", 0), ("TMP", 1), ark(SZs[h % 3])], [ark(h)])
            yield

        def interleave(*gens):
            gens = [g for g in gens if g is not None]
            while gens:
                for g in list(gens):
                    try:
                        next(g)
                        yield
                    except StopIteration:
                        gens.remove(g)

        def chain(*gens):
            for g in gens:
                yield from g

        def run(g):
            for _ in g:
                pass

        def gen_Cfull(h):
            yield from gen_C(h, 0)
            yield from gen_C(h, 1)
            yield from gen_onorm(h)

        nheads = 1 if stage == 4 else H
        run(gen_A(0))
        run(interleave(gen_A(1) if nheads > 1 else None, gen_B(0, 0), gen_B(0, 1)))
        for k in range(nheads):
            nb = k + 1 < nheads
            run(interleave(gen_A(k + 2) if k + 2 < nheads else None, gen_B(k + 1, 0) if nb else None, gen_B(k + 1, 1) if nb else None, gen_Cfull(k)))
        mD1 = S.mark()
        if os.environ.get("K_NORESCHED") != "1":
            S.reschedule(mD0, mD1)
        S.op("pool", lambda e: e.nop(), writes=coarse + list(BK[1].values()))
        if stage == 4:
            dump(OT[:], [("OT", 0), ("OT", 1)], 1024)
            CP("dve", TMP[:], ar(0), [ark(0)], [("TMP", 0), ("TMP", 1)])
            dump(TMP[:], [("TMP", 0), ("TMP", 1)], 1024)
            dump(SIL[:, 0:512], [("SILa", 0)], 512)
            break
        for _ in range(NSLOT22):
            w22_issue()
        for c in range(8):
            def evac_b(half, b):
                CP("act", SIL[:, half * 512: half * 512 + 512], PS[b][:], [psk(b)], [("SILa", half)])
            dense(32 + c, evac_b)

            def evac_c(half, b):
                CP("act", SGA[:, half * 512: half * 512 + 512], PS[b][:], [psk(b)], [("SGA", half)])
            dense(40 + c, evac_c)

            def fill(pc, i, c=c):
                def evac_x(half, b):
                    TT("dve", pc[:, 3 + half * 512: 3 + half * 512 + 512], PS[b][:], SGA[:, half * 512: half * 512 + 512], ALU.mult,
                       [psk(b), ("SGA", half)], [("PCb", i, half)])
                dense(48 + c, evac_x)
                return
                yield

            def consume(half, b, c=c):
                sl = slice(half * 512, half * 512 + 512)
                TT("dve", ar(SCs[c], half * 512, half * 512 + 512), PS[b][:], SIL[:, sl], ALU.mult, [psk(b), ("SILa", half)], [ark(SCs[c])])
            run(conv_pe(3, 24 + c, scw, c * 3, fill, consume, bank))
        for c in range(8):
            def evac_ya(half, b):
                CP("act", SIL[:, half * 512: half * 512 + 512], PS[b][:], [psk(b)], [("SILa", half)])
            dense_ar(72 + c, list(range(0, 8)), evac_ya)

            def evac_yb(half, b):
                CP("act", OT[:, half * 512: half * 512 + 512], PS[b][:], [psk(b)], [("OT", half)])
            dense_ar(80 + c, list(SCs), evac_yb)

            def evac_ga(half, b):
                sl = slice(half * 512, half * 512 + 512)
                sig_from_psum(b, half)
                TT("dve", SIL[:, sl], SIL[:, sl], SGA[:, sl], ALU.mult, [("SILa", half), ("SGA", half)], [("SILa", half)])
            dense(56 + c, evac_ga)

            def evac_gb(half, b, c=c):
                sl = slice(half * 512, half * 512 + 512)
                sig_from_psum(b, half)
                TT("dve", OT[:, sl], OT[:, sl], SGA[:, sl], ALU.mult, [("OT", half), ("SGA", half)], [("OT", half)])
                TT("dve", ar(MMs[c], half * 512, half * 512 + 512), SIL[:, sl], OT[:, sl], ALU.add, [("SILa", half), ("OT", half)], [ark(MMs[c])])
            dense(64 + c, evac_gb)
        for c in range(8):
            def evac_o(half, b, c=c):
                col = bsel * 8 + c
                sl = (half * 512, half * 512 + 512)
                STT("dve", xTs(c, *sl), PS[b][:], g1[:, col:col + 1], xTs(c, *sl), ALU.mult, ALU.add,
                    [psk(b), ("xT", c, half)] + MCK, [("xT", c, half)])
            dense_ar(88 + c, list(MMs), evac_o)
        if stage == 5:
            dump(xT[:, 0:1024], [("xT", 0, 0), ("xT", 0, 1)], 1024)
            break
        norm_mod(wm2, sh2, bsel, hTs, lambda c, hf: [("hT", c, hf)], [22, 23, 8, 9])
        split_first[0] = 4
        for j in range(NFF):
            def evac_g(half, b):
                sl = slice(half * 512, half * 512 + 512)
                sig_from_psum(b, half)
                TT("dve", SIL[:, sl], PS[b][:], SGA[:, sl], ALU.mult, [psk(b), ("SGA", half)], [("SILa", half)])
            dense(96 + j, evac_g)

            def evac_u(half, b, j=j):
                TT("dve", ar(j, half * 512, half * 512 + 512), PS[b][:], SIL[:, half * 512: half * 512 + 512], ALU.mult,
                   [psk(b), ("SILa", half)], [ark(j)])
            dense(96 + NFF + j, evac_u)
        for c in range(8):
            i22 = w22_used[0]; w22_used[0] += 1
            slot = i22 % NSLOT22
            for half in range(2):
                b = bank()
                MMG([(PS[b][:], W22[slot][:, kc * 128:(kc + 1) * 128], ar(kc, half * 512, half * 512 + 512), kc == 0, kc == NFF - 1) for kc in range(NFF)],
                    [("w22", slot)] + [ark(j) for j in range(NFF)], [psk(b)])
                col = bsel * 8 + c
                sl = (half * 512, half * 512 + 512)
                STT("dve", xTs(c, *sl), PS[b][:], g2[:, col:col + 1], xTs(c, *sl), ALU.mult, ALU.add,
                    [psk(b), ("xT", c, half)] + MCK, [("xT", c, half)])
            if c + NSLOT22 < 8:
                w22_issue()
        YT = AR[:, 0:8 * TB].bitcast(F32)
        for half in range(2):
            rms_rstd(xTs, lambda c, hf: [("xT", c, hf)], KC, onesD, half, [22, 23, 8, 9])
            lo = half * 512
            for c in range(KC):
                col = bsel * 8 + c
                tb_, tk_ = ((TMP, "TMP"), (SGA, "SGA"))[c % 2]
                STT("dve", tb_[:, lo:lo + 512], xTs(c, lo, lo + 512), wmf[:, col:col + 1], RST[:, lo:lo + 512], ALU.mult, ALU.mult,
                    [("xT", c, half), ("RST", half)] + MCK, [(tk_, half)])
                if c in (3, 7):
                    TS("dve", YT[:, c * 512:(c + 1) * 512], tb_[:, lo:lo + 512], shf[:, col:col + 1], ALU.add, [(tk_, half)] + MCK, [ark(c)])
                else:
                    ACT(YT[:, c * 512:(c + 1) * 512], tb_[:, lo:lo + 512], AF.Identity, [(tk_, half)] + MCK, [ark(c)],
                        bias=shf[:, col:col + 1], scale=1.0)
            for t4 in range(4):
                t = half * 4 + t4
                sg = t % 2
                for g in range(2):
                    b = bank()
                    TRG([(PS[b][:, q * 128:(q + 1) * 128], YT[:, (g * 4 + q) * 512 + t4 * 128: (g * 4 + q) * 512 + t4 * 128 + 128], identF) for q in range(4)],
                        [ark(g * 4 + q) for q in range(4)] + ["CF"], [psk(b)])
                    CP("act" if g == 0 else "dve", stage_x[sg][:, g * 512:(g + 1) * 512], PS[b][:], [psk(b)], [("stg", sg)])
                row0 = blk * TB + t * 128
                S.op("sp", (lambda dst, src: (lambda e: e.dma_start(out=dst, in_=src)))(out_d[row0:row0 + 128, :], stage_x[sg][:]),
                     reads=[("stg", sg)], dma=True, dmakey=("out", sg))

    if os.environ.get("K_SCHED", "ALL") == "ALL":
        S.reschedule(mALL0, S.mark())
    fin = [o for o in S.ops["sp"] if o.is_dma and (str(o.dmakey).startswith("dbg") or (isinstance(o.dmakey, tuple) and o.dmakey[0] == "out"))]
    last = S.op("sp", lambda e: e.nop())
    seen = set()
    for o in fin:
        if id(o) not in seen:
            seen.add(id(o)); last.deps.append(o)
    S.emit(st)
    st.close()
    return nc


def _prep_shared(inp):
    f = np.float32
    w_in = np.asarray(inp["w_in"], f)[0]
    def chunks(w, cols):
        out = np.empty((len(cols), 128, KC * 128), f)
        for i, c0 in enumerate(cols):
            out[i] = w[:, c0:c0 + 128].reshape(KC, 128, 128).transpose(1, 0, 2).reshape(128, KC * 128)
        return out
    cols_in = [i * 128 for i in range(32)] + [4112 + i * 128 for i in range(40)]
    w8 = np.concatenate([
        chunks(w_in, cols_in),
        chunks(np.asarray(inp["w_gdn_proj"], f)[0], [i * 128 for i in range(8)]),
        chunks(np.asarray(inp["w_sc_out"], f)[0], [i * 128 for i in range(8)]),
        chunks(np.asarray(inp["w_o"], f)[0], [i * 128 for i in range(8)]),
        chunks(np.asarray(inp["w_ffn_in"], f)[0], [i * 128 for i in range(44)]),
    ], axis=0)
    wfo = np.asarray(inp["w_ffn_out"], f)[0]
    w22 = np.empty((8, 128, NFF * 128), f)
    for c in range(8):
        w22[c] = wfo[:, c * 128:(c + 1) * 128].reshape(NFF, 128, 128).transpose(1, 0, 2).reshape(128, NFF * 128)
    wab = w_in[:, 4096:4112].reshape(KC, 128, 16).transpose(1, 0, 2).reshape(128, KC * 16)
    wada = chunks(np.asarray(inp["w_ada"], f)[0], [i * 128 for i in range(48)])
    wadaf = chunks(np.asarray(inp["w_ada_f"], f), [i * 128 for i in range(16)])
    bada = np.asarray(inp["b_ada"], f)[0].reshape(48, 128).T
    badaf = np.asarray(inp["b_ada_f"], f).reshape(16, 128).T
    nw = np.concatenate([np.asarray(inp[k], f).reshape(-1).reshape(8, 128).T for k in ("norm1_w", "norm2_w", "normf_w")], axis=1)
    gcw = np.asarray(inp["gdn_conv_w"], f)[0].reshape(4, 24, 128).transpose(2, 1, 0).reshape(128, 96)
    scw = np.asarray(inp["sc_conv_w"], f)[0].reshape(3, 8, 128).transpose(2, 1, 0).reshape(128, 24)
    hp = np.concatenate([np.broadcast_to(np.asarray(inp["gdn_a_log"], f).reshape(1, 8), (128, 8)),
                         np.broadcast_to(np.asarray(inp["gdn_dt_bias"], f).reshape(1, 8), (128, 8))], axis=1)
    gnw = np.asarray(inp["gdn_norm_w"], f).reshape(128, 1)
    cf, cbf = _masks()
    c_ = lambda a: np.ascontiguousarray(a, dtype=f)
    return dict(w8=c_(w8), w22=c_(w22), wab=c_(wab), wada=c_(wada), wadaf=c_(wadaf), bada=c_(bada), badaf=c_(badaf),
                nw=c_(nw), gcw=c_(gcw), scw=c_(scw), hp=c_(hp), gnw=c_(gnw), cf=cf, cbf=cbf)


def _in_maps(inp, ncores=NCORES):
    shared = _prep_shared(inp)
    x = np.asarray(inp["x"], np.float32)
    c = np.asarray(inp["c"], np.float32)
    maps = []
    for i in range(ncores):
        m = dict(shared)
        m["x"] = np.ascontiguousarray(x[2 * i:2 * i + 2].reshape(2 * S_LEN, D))
        cc = c[2 * i:2 * i + 2]
        m["cT"] = np.ascontiguousarray(cc.reshape(2, KC, 128).transpose(2, 1, 0).reshape(128, KC * 2))
        maps.append(m)
    return maps


def kernel(**inputs):
    nc = build_nc()
    maps = _in_maps(inputs)
    res = run_bass_kernel_spmd(nc, maps, core_ids=list(range(NCORES)))
    out = np.empty((16, S_LEN, D), np.float32)
    for i in range(NCORES):
        out[2 * i:2 * i + 2] = np.asarray(res.results[i]["out"], np.float32).reshape(2, S_LEN, D)
    return out
```

```python
import os
from contextlib import ExitStack
import numpy as np
import concourse.bass as bass
import concourse.mybir as mybir
from concourse.bass_utils import run_bass_kernel_spmd

F32 = mybir.dt.float32
BF16 = mybir.dt.bfloat16
ALU = mybir.AluOpType
AF = mybir.ActivationFunctionType

NCORES = 8
D = 1024
KC = 8
S_LEN = 2048
TB = 1024
NB = 4
NT = TB // 128
H = 8
DFF = 2816
NFF = DFF // 128
EPS = 1e-6
NW8 = 72 + 8 + 8 + 8 + 44
NSLOT8 = 4
NSLOT22 = 2
NAR = 24


class Op:
    __slots__ = ("eng", "fn", "deps", "is_dma", "sem", "val", "needs_inc", "dmakey", "seq", "cost", "pdeps", "prio", "tend")

    def __init__(self, eng, fn, is_dma, dmakey):
        self.eng = eng; self.fn = fn; self.deps = []; self.is_dma = is_dma
        self.sem = None; self.val = None; self.needs_inc = False; self.dmakey = dmakey
        self.seq = 0; self.cost = 0.3; self.pdeps = []; self.prio = 0.0; self.tend = 0.0


class Sched:
    ENGS = ("pe", "dve", "act", "pool", "sp")

    def __init__(self, nc):
        self.nc = nc
        self.ops = {e: [] for e in self.ENGS}
        self.last_w = {}
        self.readers = {}

    def mark(self):
        return {e: len(self.ops[e]) for e in self.ENGS}

    def reschedule(self, m0, m1):
        import heapq
        region = []
        for e in self.ENGS:
            region += self.ops[e][m0[e]:m1[e]]
        inreg = set(id(o) for o in region)
        LAT = 0.2
        region.sort(key=lambda o: o.seq)
        succ = {id(o): [] for o in region}
        npred = {}
        for o in region:
            ds = [d for d in (o.deps + o.pdeps) if id(d) in inreg]
            npred[id(o)] = len(ds)
            for d in ds:
                succ[id(d)].append(o)
        for o in reversed(region):
            best = 0.0
            for c in succ[id(o)]:
                best = max(best, c.prio + LAT)
            dl = 6.0 if o.is_dma else 0.0
            o.prio = o.cost + dl + best
        ready_t = {id(o): 0.0 for o in region}
        avail = {e: [] for e in self.ENGS}
        for o in region:
            if npred[id(o)] == 0:
                heapq.heappush(avail[o.eng], (-o.prio, o.seq, o))
        eng_t = {e: 0.0 for e in self.ENGS}
        order = {e: [] for e in self.ENGS}
        remaining = len(region)
        while remaining:
            best = None
            for e in self.ENGS:
                h = avail[e]
                if not h:
                    continue
                cands = heapq.nsmallest(6, h)
                tmin = min(max(eng_t[e], ready_t[id(c[2])]) for c in cands)
                pick = None
                for c in cands:
                    st = max(eng_t[e], ready_t[id(c[2])])
                    if st <= tmin + 0.3:
                        pick = c
                        break
                st = max(eng_t[e], ready_t[id(pick[2])])
                if best is None or st < best[0]:
                    best = (st, e, pick)
            st, e, pick = best
            avail[e].remove(pick)
            heapq.heapify(avail[e])
            o = pick[2]
            eng_t[e] = st + o.cost
            o.tend = st + o.cost + (6.0 if o.is_dma else 0.0)
            order[e].append(o)
            remaining -= 1
            for c in succ[id(o)]:
                ready_t[id(c)] = max(ready_t[id(c)], o.tend + (LAT if c.eng != e else 0.1))
                npred[id(c)] -= 1
                if npred[id(c)] == 0:
                    heapq.heappush(avail[c.eng], (-c.prio, c.seq, c))
        for e in self.ENGS:
            assert len(order[e]) == m1[e] - m0[e]
            self.ops[e][m0[e]:m1[e]] = order[e]
        return max(eng_t.values())

    def op(self, eng, fn, reads=(), writes=(), dma=False, dmakey=None, cost=None):
        o = Op(eng, fn, dma, dmakey)
        self.nseq = getattr(self, "nseq", 0) + 1
        o.seq = self.nseq
        if cost is not None:
            o.cost = cost
        elif dma:
            o.cost = 1.0 if eng == "pool" else 0.3
        deps = []
        for k in reads:
            w = self.last_w.get(k)
            if w is not None:
                deps.append(w)
            self.readers.setdefault(k, []).append(o)
        for k in writes:
            w = self.last_w.get(k)
            if w is not None:
                deps.append(w)
            deps.extend(self.readers.get(k, ()))
            self.last_w[k] = o
            self.readers[k] = []
        seen = set()
        for d in deps:
            if d is o or id(d) in seen:
                continue
            if d.eng == "pe" and eng == "pe" and not d.is_dma and not dma:
                if id(d) not in seen:
                    seen.add(id(d)); o.pdeps.append(d)
                continue
            seen.add(id(d))
            o.deps.append(d)
            d.needs_inc = True
        self.ops[eng].append(o)
        return o

    def emit(self, stack):
        nc = self.nc
        ROT = 12000
        lp = None
        for o in self.ops["pool"]:
            if o.is_dma:
                continue
            if lp is not None and all(d is not lp for d in o.deps):
                o.deps.append(lp)
                lp.needs_inc = True
            lp = o
        dma_sems = {}
        nsem = [0]

        def newsem(tag):
            nsem[0] += 1
            return stack.enter_context(nc.semaphore("s%d_%s" % (nsem[0], tag)))

        for e in self.ENGS:
            cur = None; cnt = 0
            for o in self.ops[e]:
                if o.is_dma:
                    key = o.dmakey
                    if key not in dma_sems:
                        dma_sems[key] = [newsem("d"), 0]
                    ent = dma_sems[key]
                    ent[1] += 16
                    if ent[1] > 30000:
                        ent[0] = newsem("d"); ent[1] = 16
                    o.sem = ent[0]; o.val = ent[1]
                elif o.needs_inc:
                    if cur is None or cnt >= ROT:
                        cur = newsem(e); cnt = 0
                    cnt += 1
                    o.sem = cur; o.val = cnt
        engmap = {"pe": "tensor", "dve": "vector", "act": "scalar", "pool": "gpsimd", "sp": "sync"}
        block = stack.enter_context(nc.Block())
        for e in self.ENGS:
            ops = self.ops[e]

            def body(eng, ops=ops):
                waited = {}
                for o in ops:
                    for d in o.deps:
                        sid = id(d.sem)
                        if waited.get(sid, 0) >= d.val:
                            continue
                        eng.wait_ge(d.sem, d.val)
                        waited[sid] = d.val
                    ins = o.fn(eng)
                    if o.is_dma:
                        ins.then_inc(o.sem, 16)
                    elif o.needs_inc:
                        ins.then_inc(o.sem, 1)
            getattr(block, engmap[e])(body)


def _masks():
    idx = np.arange(128)
    same = (idx[:, None] // 64) == (idx[None, :] // 64)
    m2 = (same & (idx[:, None] <= idx[None, :])).astype(np.float32)
    ms = (same & (idx[None, :] < idx[:, None])).astype(np.float32)
    m2t = np.ascontiguousarray(m2.T)
    ca = np.repeat((idx < 64).astype(np.float32)[:, None], 128, axis=1)
    cb = np.repeat((idx >= 64).astype(np.float32)[:, None], 128, axis=1)
    ident = np.eye(128, dtype=np.float32)
    ones = np.ones((128, 128), np.float32)
    cf = np.concatenate([ident, ones, m2, ms, m2t, ca, cb], axis=1)
    mls = []
    for sz in (1, 2, 4, 8, 16, 32):
        blk = (idx[:, None] // (2 * sz)) == (idx[None, :] // (2 * sz))
        mls.append((blk & ((idx[:, None] % (2 * sz)) >= sz) & ((idx[None, :] % (2 * sz)) < sz)).astype(np.float32))
    cbf = np.concatenate([ident, ones / 1024.0, ones, ones / 128.0] + mls + [np.ascontiguousarray(mls[0].T)], axis=1)
    return np.ascontiguousarray(cf), np.ascontiguousarray(cbf)


def build_nc(stage=99, nblocks=NB, dbg_cols=0):
    nc = bass.Bass("TRN2", target_bir_lowering=False)
    dr = lambda name, shape, dt=F32, kind="ExternalInput": nc.dram_tensor(name, shape, dt, kind=kind).ap()
    x_d = dr("x", [NB * TB, D])
    out_d = dr("out", [NB * TB, D], kind="ExternalOutput")
    cT_d = dr("cT", [128, KC * 2])
    wada_d = dr("wada", [48, 128, KC * 128])
    bada_d = dr("bada", [128, 48])
    wadaf_d = dr("wadaf", [16, 128, KC * 128])
    badaf_d = dr("badaf", [128, 16])
    nw_d = dr("nw", [128, 24])
    gcw_d = dr("gcw", [128, 24 * 4])
    scw_d = dr("scw", [128, 8 * 3])
    hp_d = dr("hp", [128, 16])
    gnw_d = dr("gnw", [128, 1])
    wab_d = dr("wab", [128, KC * 16])
    w8_d = dr("w8", [NW8, 128, KC * 128])
    w22_d = dr("w22", [8, 128, NFF * 128])
    cf_d = dr("cf", [128, 7 * 128])
    cbf_d = dr("cbf", [128, 11 * 128])
    dbg_d = dr("dbg", [128, max(dbg_cols, 1)], kind="ExternalOutput") if dbg_cols else None

    st = ExitStack()
    sb = lambda name, shape, dt=F32: st.enter_context(nc.sbuf_tensor(name, shape, dt))
    CF = sb("CF", [128, 7 * 128]); CB = sb("CBc", [128, 11 * 128], BF16)
    identF = CF[:, 0:128]; onesF = CF[:, 128:256]; M2 = CF[:, 256:384]; MS = CF[:, 384:512]
    M2T = CF[:, 512:640]; CAm = CF[:, 640:768]; CBm = CF[:, 768:896]
    identB = CB[:, 0:128]; onesD = CB[:, 128:256]; onesK = CB[:, 256:384]; onesV = CB[:, 384:512]
    MLm = [CB[:, (4 + i) * 128:(5 + i) * 128] for i in range(6)]; MU1 = CB[:, 10 * 128:11 * 128]
    cT = sb("cTs", [128, KC * 2]); cA = sb("cA", [128, KC * 2])
    bada = sb("badas", [128, 48]); badaf = sb("badafs", [128, 16])
    modT = sb("modT", [128, 48 * 2]); modF = sb("modF", [128, 16 * 2])
    nw = sb("nws", [128, 24]); gcw = sb("gcws", [128, 96]); scw = sb("scws", [128, 24])
    hp = sb("hps", [128, 16]); negA = sb("negA", [128, 8]); gnw = sb("gnws", [128, 1])
    wab = sb("wabs", [128, KC * 16], BF16)
    wm1 = sb("wm1", [128, 16]); sh1 = sb("sh1", [128, 16]); g1 = sb("g1", [128, 16])
    wm2 = sb("wm2", [128, 16]); sh2 = sb("sh2", [128, 16]); g2 = sb("g2", [128, 16])
    wmf = sb("wmf", [128, 16]); shf = sb("shf", [128, 16])
    xT = sb("xT", [128, KC * TB])
    hT = sb("hT", [128, KC * TB], BF16)
    AR = sb("AR", [128, NAR * TB], BF16)
    stage_x = [sb("stg%d" % i, [128, D]) for i in range(2)]
    wst = stage_x
    PCb = [sb("PCb%d" % i, [128, TB + 3], BF16) for i in range(2)]; DG = [sb("DG%d" % i, [128, 512], BF16) for i in range(2)]
    SIL = sb("SIL", [128, TB]); SGA = sb("SGA", [128, TB]); SQC = sb("SQC", [128, TB], BF16)
    RST = sb("RST", [128, TB])
    halo = sb("halo", [128, 32 * 3], BF16)
    kbeg = [sb("kbeg%d" % i, [128, NT * 128], BF16) for i in range(2)]; kdec = [sb("kdec%d" % i, [128, NT * 128], BF16) for i in range(2)]
    vb = [sb("vb%d" % i, [128, NT * 128], BF16) for i in range(2)]
    W8 = [sb("W8_%d" % i, [128, KC * 128], BF16) for i in range(NSLOT8)]
    W22 = [sb("W22_%d" % i, [128, NFF * 128], BF16) for i in range(NSLOT22)]
    ab = sb("ab", [128, NT * 16]); a1 = sb("a1", [128, NT * 8]); graw = sb("graw", [128, NT * 8])
    beta = sb("beta", [128, NT * 8]); EX = sb("EX", [128, NT * 32]); beg = sb("beg", [128, NT * 8])
    Sst = sb("Sst", [128, H * 128]); Sbf = sb("Sbf", [128, H * 128], BF16)
    GM = sb("GM", [128, 512]); SBm = sb("SBm", [128, 512], BF16); Dm = sb("Dm", [128, 512], BF16)
    DD = sb("DD", [128, 512], BF16)
    Lb = sb("Lb", [128, 512], BF16); Ub = sb("Ub", [128, 512], BF16); QKb = sb("QKb", [128, 512], BF16)
    QKT = [sb("QKT%d" % i, [128, 512], BF16) for i in range(2)]
    Pb = [sb("Pb%d" % i, [128, 512], BF16) for i in range(2)]
    Qb = [sb("Qb%d" % i, [128, 512], BF16) for i in range(2)]
    Xb = [sb("Xb%d" % i, [128, 512], BF16) for i in range(2)]
    EGB = sb("EGB", [128, 512], BF16); QG = [sb("QG%d" % i, [128, 512], BF16) for i in range(2)]
    Usb = [sb("Usb%d" % i, [128, 512], BF16) for i in range(2)]; NWT = [sb("NWT%d" % i, [128, 512], BF16) for i in range(2)]
    VN = sb("VN", [128, 128], BF16)
    OT = sb("OT", [128, TB])
    TMP = sb("TMP", [128, TB])
    PS = [st.enter_context(nc.psum_tensor("PS%d" % i, [128, 512], F32)) for i in range(8)]

    S = Sched(nc)

    def _n(ap):
        try:
            return int(ap.free_size())
        except Exception:
            return 512

    def ACT(out, in_, func, reads, writes, bias=None, scale=None):
        kw = {}
        if bias is not None:
            kw["bias"] = bias
        if scale is not None:
            kw["scale"] = scale
        return S.op("act", lambda e: e.activation(out=out, in_=in_, func=func, **kw), reads=reads, writes=writes, cost=0.22 + _n(out) / 1150.0)

    def _vc(eng, out):
        return (0.2 + _n(out) / 1000.0) if eng == "dve" else (0.65 + _n(out) / 600.0)

    def TT(eng, out, in0, in1, op, reads, writes):
        return S.op(eng, lambda e: e.tensor_tensor(out=out, in0=in0, in1=in1, op=op), reads=reads, writes=writes, cost=_vc(eng, out))

    def STT(eng, out, in0, scalar, in1, op0, op1, reads, writes):
        return S.op(eng, lambda e: e.scalar_tensor_tensor(out=out, in0=in0, scalar=scalar, in1=in1, op0=op0, op1=op1), reads=reads, writes=writes,
                    cost=_vc(eng, out))

    def TS(eng, out, in0, s1, op0, reads, writes):
        return S.op(eng, lambda e: e.tensor_scalar(out=out, in0=in0, scalar1=s1, scalar2=None, op0=op0), reads=reads, writes=writes, cost=_vc(eng, out))

    def CP(eng, out, in_, reads, writes):
        if eng == "act":
            return S.op("act", lambda e: e.activation(out=out, in_=in_, func=AF.Copy), reads=reads, writes=writes, cost=0.22 + _n(out) / 1150.0)
        return S.op(eng, lambda e: e.tensor_copy(out=out, in_=in_), reads=reads, writes=writes, cost=_vc(eng, out))

    def MMG(lst, reads, writes):
        lst = list(lst)
        cost = sum(0.06 + max(_n(r_), 32) / 2800.0 * (4 if l_.dtype == F32 else 1) for (o_, l_, r_, s0, s1) in lst)

        def f(e):
            ins = None
            for (o_, l_, r_, s0, s1) in lst:
                ins = e.matmul(o_, lhsT=l_, rhs=r_, start=s0, stop=s1)
            return ins
        return S.op("pe", f, reads=reads, writes=writes, cost=cost)

    def TRG(lst, reads, writes):
        lst = list(lst)

        def f(e):
            ins = None
            for (o_, i_, id_) in lst:
                ins = e.transpose(o_, i_, id_)
            return ins
        return S.op("pe", f, reads=reads, writes=writes, cost=0.11 * len(lst))

    psk = lambda i: ("ps", i)
    bank_ctr = [0]

    def bank():
        b = bank_ctr[0] % 8
        bank_ctr[0] += 1
        return b

    def ar(slot, lo=0, hi=TB):
        if slot == "SQC":
            return SQC[:, lo:hi]
        if slot == "TMPB":
            return TMP[:].bitcast(BF16)[:, lo:hi]
        return AR[:, slot * TB + lo: slot * TB + hi]

    def ark(slot):
        if slot == "TMPB":
            return ("TMP", 0)
        return ("ar", slot)

    def xTs(c, lo, hi):
        return xT[:, c * TB + lo: c * TB + hi]

    def hTs(c, lo, hi):
        return hT[:, c * TB + lo: c * TB + hi]

    dbg_off = [0]

    def dump(ap, key, ncols):
        if dbg_d is None:
            return
        o = dbg_off[0]
        dbg_off[0] += ncols
        keys = key if isinstance(key, list) else [key]
        S.op("sp", lambda e: e.dma_start(out=dbg_d[:, o:o + ncols], in_=ap), reads=keys, dma=True, dmakey="dbg%d" % o)

    def ld(dst, src, key, eng="sp"):
        S.op(eng, lambda e: e.dma_start(out=dst, in_=src), writes=[key], dma=True, dmakey=key)

    ld(CF[:], cf_d, "CF"); ld(CB[:], cbf_d, "CB", "pool"); ld(cT[:], cT_d, "cT"); ld(bada[:], bada_d, "bada")
    ld(badaf[:], badaf_d, "badaf"); ld(nw[:], nw_d, "nw"); ld(gcw[:], gcw_d, "gcw"); ld(scw[:], scw_d, "scw")
    ld(hp[:], hp_d, "hp"); ld(gnw[:], gnw_d, "gnw"); ld(wab[:], wab_d, "wab", "pool")
    EPSC = sb("EPSC", [128, 1]); ONEC = sb("ONEC", [128, 1])
    S.op("dve", lambda e: e.memset(EPSC[:], EPS), writes=["EPSC"])
    S.op("dve", lambda e: e.memset(ONEC[:], 1.0), writes=["ONEC"])

    ACT(cA[:], cT[:], AF.Silu, ["cT"], ["cA"])
    ACT(negA[:], hp[:, 0:8], AF.Exp, ["hp"], ["negA"])
    TS("dve", negA[:], negA[:], -1.0, ALU.mult, ["negA"], ["negA"])

    cAb = sb("cAb", [128, KC * 2], BF16)
    CP("dve", cAb[:], cA[:], ["cA"], ["cAb"])

    def adaln(w_dram, nchunk, outT, bias_t, bkey, tag):
        for j in range(nchunk):
            slot = j % NSLOT8
            src = w_dram[j]
            S.op("pool", (lambda dst, src: (lambda e: e.dma_start(out=dst, in_=src)))(W8[slot][:], src),
                 writes=[("w8", slot)], dma=True, dmakey=("w8", slot))
            b = bank()
            MMG([(PS[b][:, 0:2], W8[slot][:, kc * 128:(kc + 1) * 128], cAb[:, kc * 2:(kc + 1) * 2], kc == 0, kc == KC - 1) for kc in range(KC)],
                [("w8", slot), "cAb"], [psk(b)])
            TS("dve", outT[:, j * 2:(j + 1) * 2], PS[b][:, 0:2], bias_t[:, j:j + 1], ALU.add, [psk(b), bkey], [tag])

    adaln(wada_d, 48, modT, bada, "bada", "mod")
    adaln(wadaf_d, 16, modF, badaf, "badaf", "modf")
    modT3 = modT[:].rearrange("p (j b) -> p j b", b=2)
    modF3 = modF[:].rearrange("p (j b) -> p j b", b=2)

    def modcols(dst, src3, kind):
        for b_ in range(2):
            CP("dve", dst[:, b_ * 8:(b_ + 1) * 8], src3[:, kind * 8:(kind + 1) * 8, b_], ["mod", "modf"], [("mc", id(dst))])

    def wmcols(dst, src3, kind, nwoff):
        for b_ in range(2):
            STT("dve", dst[:, b_ * 8:(b_ + 1) * 8], src3[:, kind * 8:(kind + 1) * 8, b_], 1.0, nw[:, nwoff:nwoff + 8], ALU.add, ALU.mult,
                ["mod", "modf", "nw"], [("mc", id(dst))])

    modcols(sh1, modT3, 0); wmcols(wm1, modT3, 1, 0); modcols(g1, modT3, 2)
    modcols(sh2, modT3, 3); wmcols(wm2, modT3, 4, 8); modcols(g2, modT3, 5)
    modcols(shf, modF3, 0); wmcols(wmf, modF3, 1, 16)
    MCK = [("mc", id(t)) for t in (sh1, wm1, g1, sh2, wm2, g2, shf, wmf)]

    wl8 = []
    for h in range(H):
        wl8 += [h, 8 + h, 16 + h, 24 + h]
    for c in range(8):
        wl8 += [32 + c, 40 + c, 48 + c]
    for c in range(8):
        wl8 += [72 + c, 80 + c, 56 + c, 64 + c]
    for c in range(8):
        wl8 += [88 + c]
    for j in range(NFF):
        wl8 += [96 + j, 96 + NFF + j]
    assert len(wl8) == NW8
    w8_issued = [0]; w8_used = [0]
    tot8 = nblocks * NW8

    def w8_issue():
        i = w8_issued[0]
        if i >= tot8:
            return
        slot = i % NSLOT8
        dst = W8[slot][:]; src = w8_d[wl8[i % NW8]]
        S.op("pool", lambda e: e.dma_start(out=dst, in_=src), writes=[("w8", slot)], dma=True, dmakey=("w8", slot))
        w8_issued[0] += 1

    def w8_next(expect):
        i = w8_used[0]
        assert wl8[i % NW8] == expect, (i, wl8[i % NW8], expect)
        w8_used[0] += 1
        return i % NSLOT8

    w22_issued = [0]; w22_used = [0]
    tot22 = nblocks * 8

    def w22_issue():
        i = w22_issued[0]
        if i >= tot22:
            return
        slot = i % NSLOT22
        dst = W22[slot][:]; src = w22_d[i % 8]
        S.op("pool", lambda e: e.dma_start(out=dst, in_=src), writes=[("w22", slot)], dma=True, dmakey=("w22", slot))
        w22_issued[0] += 1

    for _ in range(NSLOT8 - 1):
        w8_issue()

    split_first = [0]

    def dense(widx, evac, bk=None):
        bk = bk or bank
        w8_issue()
        slot = w8_next(widx)
        for half in range(2):
            b = bk()
            if split_first[0] > 0:
                split_first[0] -= 1
                for kc in range(KC):
                    MMG([(PS[b][:], W8[slot][:, kc * 128:(kc + 1) * 128], hTs(kc, half * 512, half * 512 + 512), kc == 0, kc == KC - 1)],
                        [("w8", slot), ("hT", kc, half)], [psk(b)])
            else:
                MMG([(PS[b][:], W8[slot][:, kc * 128:(kc + 1) * 128], hTs(kc, half * 512, half * 512 + 512), kc == 0, kc == KC - 1) for kc in range(KC)],
                    [("w8", slot)] + [("hT", kc, half) for kc in range(KC)], [psk(b)])
            evac(half, b)

    def dense_ar(widx, slots, evac):
        w8_issue()
        slot = w8_next(widx)
        for half in range(2):
            b = bank()
            MMG([(PS[b][:], W8[slot][:, kc * 128:(kc + 1) * 128], ar(slots[kc], half * 512, half * 512 + 512), kc == 0, kc == KC - 1) for kc in range(KC)],
                [("w8", slot)] + [ark(s_) for s_ in slots], [psk(b)])
            evac(half, b)

    def rms_rstd(src_ap_fn, src_keys_fn, nch, ones_ap, half, sq_slots, bk=None):
        lo = half * 512
        sqap = lambda c: ar(sq_slots[c // 2], (c % 2) * 512, (c % 2) * 512 + 512)
        for c in range(nch):
            if nch == KC and c in (1, 4, 7):
                TT("pool", sqap(c), src_ap_fn(c, lo, lo + 512), src_ap_fn(c, lo, lo + 512), ALU.mult, src_keys_fn(c, half), [ark(sq_slots[c // 2])])
            else:
                ACT(sqap(c), src_ap_fn(c, lo, lo + 512), AF.Square, src_keys_fn(c, half), [ark(sq_slots[c // 2])])
        b = (bk or bank)()
        MMG([(PS[b][:], ones_ap, sqap(c), c == 0, c == nch - 1) for c in range(nch)],
            [ark(s_) for s_ in sq_slots[:(nch + 1) // 2]] + ["CB"], [psk(b)])
        ACT(RST[:, lo:lo + 512], PS[b][:], AF.Ln, [psk(b), "EPSC"], [("RST", half)], bias=EPSC[:, 0:1], scale=1.0)
        ACT(RST[:, lo:lo + 512], RST[:, lo:lo + 512], AF.Exp, [("RST", half)], [("RST", half)], scale=-0.5)

    def norm_mod(wm, sh, bsel, dst_fn, dst_keys_fn, sq_slots):
        for half in range(2):
            rms_rstd(xTs, lambda c, hf: [("xT", c, hf)], KC, onesD, half, sq_slots)
            lo = half * 512
            for c in range(KC):
                col = bsel * 8 + c
                tb_, tk_ = ((TMP, "TMP"), (SGA, "SGA"))[c % 2]
                STT("dve", tb_[:, lo:lo + 512], xTs(c, lo, lo + 512), wm[:, col:col + 1], RST[:, lo:lo + 512], ALU.mult, ALU.mult,
                    [("xT", c, half), ("RST", half)] + MCK, [(tk_, half)])
                if c in (3, 7):
                    TS("dve", dst_fn(c, lo, lo + 512), tb_[:, lo:lo + 512], sh[:, col:col + 1], ALU.add, [(tk_, half)] + MCK, dst_keys_fn(c, half))
                else:
                    ACT(dst_fn(c, lo, lo + 512), tb_[:, lo:lo + 512], AF.Identity, [(tk_, half)] + MCK, dst_keys_fn(c, half),
                        bias=sh[:, col:col + 1], scale=1.0)

    g3 = graw[:].rearrange("p (t k) -> p t k", k=8)
    b3 = beta[:].rearrange("p (t k) -> p t k", k=8)
    beg3 = beg[:].rearrange("p (t k) -> p t k", k=8)
    EX4 = EX[:].rearrange("p (t k h) -> p t k h", k=4, h=8)
    ab3 = ab[:].rearrange("p (t k) -> p t k", k=16)
    a13 = a1[:].rearrange("p (t k) -> p t k", k=8)
    r4 = lambda ap: ap.rearrange("p (a i) -> p a i", a=4)
    bc4 = lambda ap: ap.unsqueeze(1).to_broadcast([128, 4, 128])
    xT3 = xT[:].rearrange("p (c t) -> p c t", c=KC)
    SQs, SKs, SZs, STMP, STMP2 = (16, 17), (18, 19), (20, 21, 'SQC'), 22, 23
    SQCslot = 'SQC'
    SCs = (16, 17, 18, 19, 22, 23, 8, 9)
    MMs = (10, 11, 12, 13, 14, 15, 20, 21)

    mALL0 = S.mark()
    for blk in range(nblocks):
        seq = blk // 2
        first = (blk % 2 == 0)
        bsel = seq
        for t in range(NT):
            row0 = blk * TB + t * 128
            if t < 4:
                stg = hT[:, t * 2048:(t + 1) * 2048].bitcast(F32)
                skeys = [("hT", 2 * t + a_, b_) for a_ in range(2) for b_ in range(2)]
                dkey = ("xh", t)
            elif t == 4:
                stg = OT[:]; skeys = [("OT", 0), ("OT", 1)]; dkey = ("xh", t)
            elif t == 5:
                stg = SIL[:]; skeys = [("SILa", 0), ("SILa", 1)]; dkey = ("xh", t)
            else:
                sg = t % 2
                stg = stage_x[sg][:]
                skeys = [("stg", sg)]
                dkey = ("stg", sg)
            S.op("sp", (lambda dst, src: (lambda e: e.dma_start(out=dst, in_=src)))(stg, x_d[row0:row0 + 128, :]),
                 writes=skeys, dma=True, dmakey=dkey)
            for g in range(2):
                b = bank()
                TRG([(PS[b][:, q * 128:(q + 1) * 128], stg[:, (g * 4 + q) * 128:(g * 4 + q + 1) * 128], identF) for q in range(4)],
                    skeys + ["CF"], [psk(b)])
                CP("act" if g == 0 else "dve", xT3[:, g * 4:(g + 1) * 4, t * 128:(t + 1) * 128], r4(PS[b][:]),
                   [psk(b)], [("xT", g * 4 + q, t // 4) for q in range(4)])
        if stage == 1:
            dump(xT[:, 0:2048], [("xT", c_, h_) for c_ in range(2) for h_ in range(2)], 2048)
            break
        norm_mod(wm1, sh1, bsel, hTs, lambda c, hf: [("hT", c, hf)], [16, 17, 18, 19])
        if stage == 2:
            CP("dve", TMP[:], hT[:, 0:1024], [("hT", 0, 0), ("hT", 0, 1)], [("TMP", 0), ("TMP", 1)])
            dump(TMP[:], [("TMP", 0), ("TMP", 1)], 1024)
            break
        b = bank()
        for hf_ in range(2):
            MMG([(PS[b][:, t * 16:(t + 1) * 16], hTs(kc, t * 128, (t + 1) * 128), wab[:, kc * 16:(kc + 1) * 16], kc == 0, kc == KC - 1)
                 for t in range(hf_ * 4, hf_ * 4 + 4) for kc in range(KC)],
                [("hT", kc, hf_) for kc in range(KC)] + ["wab"], [psk(b)])
        CP("act", ab[:], PS[b][:, 0:NT * 16], [psk(b)], ["ab"])
        TT("dve", a13, ab3[:, :, 0:8], hp[:, 8:16].unsqueeze(1).to_broadcast([128, NT, 8]), ALU.add, ["ab", "hp"], ["a1"])
        ACT(a1[:], a1[:], AF.Exp, ["a1"], ["a1"])
        ACT(a1[:], a1[:], AF.Ln, ["a1", "ONEC"], ["a1"], bias=ONEC[:, 0:1], scale=1.0)
        TT("dve", g3, a13, negA[:].unsqueeze(1).to_broadcast([128, NT, 8]), ALU.mult, ["a1", "negA"], ["graw"])
        ACT(b3, ab3[:, :, 8:16], AF.Exp, ["ab"], ["beta"], scale=-1.0)
        TS("dve", beta[:], beta[:], 1.0, ALU.add, ["beta"], ["beta"])
        S.op("dve", lambda e: e.reciprocal(out=beta[:], in_=beta[:]), reads=["beta"], writes=["beta"])
        b = bank()
        MMG([(PS[b][:, t * 32 + k * 8: t * 32 + k * 8 + 8], lt, graw[:, t * 8:(t + 1) * 8], True, True)
             for t in range(NT) for k, lt in enumerate((M2, MS, CAm, CBm))], ["graw", "CF"], [psk(b)])
        ACT(EX[:], PS[b][:, 0:NT * 32], AF.Exp, [psk(b)], ["EX"])
        TT("dve", beg3, b3, EX4[:, :, 0, :], ALU.mult, ["beta", "EX"], ["beg"])
        if stage == 3:
            CP("dve", TMP[:, 0:64], graw[:], ["graw"], [("TMP", 0)])
            CP("dve", TMP[:, 64:128], beta[:], ["beta", ("TMP", 0)], [("TMP", 0)])
            CP("dve", TMP[:, 128:384], EX[:], ["EX", ("TMP", 0)], [("TMP", 0)])
            dump(TMP[:, 0:384], ("TMP", 0), 384)
            break
        if first:
            S.op("pool", lambda e: e.memset(Sst[:], 0.0), writes=[("S", h) for h in range(H)])
            S.op("pool", lambda e: e.memset(Sbf[:], 0.0), writes=[("Sbf", h) for h in range(H)])
            S.op("pool", lambda e: e.memset(halo[:], 0.0), writes=[("halo", i) for i in range(32)])

        KBt = [(kbeg[i][:], kdec[i][:], vb[i][:], ("kbeg", i), ("kdec", i), ("vb", i)) for i in range(2)] + \
              [(ar(8), ar(9), ar(10), ark(8), ark(9), ark(10))]
        RSt = [(QG[i][:], QKT[i][:], NWT[i][:], Usb[i][:], ("QG", i), ("QKT", i), ("NWT", i), ("Usb", i)) for i in range(2)] + \
              [(ar(s_, 0, 512), ar(s_, 512, 1024), ar(s_ + 1, 0, 512), ar(s_ + 1, 512, 1024), ark(s_), ark(s_), ark(s_ + 1), ark(s_ + 1)) for s_ in (11, 13)]

        def _tiles(ap_bf, n):
            return [ap_bf[:, i * 512:(i + 1) * 512] for i in range(n)]
        t2 = _tiles(W22[0][:], 5) + _tiles(W22[1][:], 5) + _tiles(stage_x[0][:].bitcast(BF16)[:, 1024:2048], 2) + _tiles(stage_x[1][:].bitcast(BF16), 4)
        BS = [dict(GM=GM[:], SBm=SBm[:], Dm=Dm[:], DD=DD[:], Lb=Lb[:], Ub=Ub[:], QKb=QKb[:], P0=Pb[0][:], P1=Pb[1][:], Q0=Qb[0][:], Q1=Qb[1][:],
                   X0=Xb[0][:], X1=Xb[1][:], EGB=EGB[:], LOA=ar(15, 0, 512), LOB=ar(15, 512, 1024)),
              dict(GM=stage_x[0][:, 0:512], SBm=t2[0], Dm=t2[1], DD=t2[2], Lb=t2[3], Ub=t2[4], QKb=t2[5], P0=t2[6], P1=t2[7], Q0=t2[8], Q1=t2[9],
                   X0=t2[10], X1=t2[11], EGB=t2[12], LOA=t2[13], LOB=t2[14])]
        BK = [{n: n for n in BS[0]}, {n: ("B2", n) for n in BS[1]}]
        BK[0]["LOA"] = ark(15); BK[0]["LOB"] = ark(15)
        coarse = [("w22", 0), ("w22", 1), ("stg", 0), ("stg", 1)]
        S.op("pool", lambda e: e.nop(), writes=coarse + list(BK[1].values()))
        mD0 = S.mark()

        bkA = [0]; bkB = [0]

        def bankA():
            bkA[0] += 1
            return (0, 1, 2)[bkA[0] % 3]

        def bankB():
            bkB[0] += 1
            return (3, 4)[bkB[0] % 2]

        def sig_from_psum(b, half):
            sl = slice(half * 512, half * 512 + 512)
            ACT(SGA[:, sl], PS[b][:], AF.Exp, [psk(b)], [("SGA", half)], scale=-1.0)
            ACT(SGA[:, sl], SGA[:, sl], AF.Ln, [("SGA", half), "ONEC"], [("SGA", half)], bias=ONEC[:, 0:1], scale=1.0)
            ACT(SGA[:, sl], SGA[:, sl], AF.Exp, [("SGA", half)], [("SGA", half)], scale=-1.0)

        pcc = [0]

        def conv_pe(width, hidx, wt, wcol0, fill, consume, bk):
            i = pcc[0] % 2
            pcc[0] += 1
            pc, dg = PCb[i], DG[i]
            off = 3 - (width - 1)
            for j in range(width):
                ACT(dg[:, j * 128:(j + 1) * 128], identB, AF.Identity, ["CB", "gcw", "scw"], [("DG", i)], scale=wt[:, wcol0 + j:wcol0 + j + 1])
            CP("pool", pc[:, 0:3], halo[:, hidx * 3:hidx * 3 + 3], [("halo", hidx)], [("PCb", i, "h")])
            yield from fill(pc, i)
            CP("pool", halo[:, hidx * 3:hidx * 3 + 3], pc[:, TB:TB + 3], [("PCb", i, 1)], [("halo", hidx)])
            for half in range(2):
                b = bk()
                MMG([(PS[b][:], dg[:, j * 128:(j + 1) * 128], pc[:, off + j + half * 512: off + j + half * 512 + 512], j == 0, j == width - 1)
                     for j in range(width)], [("DG", i), ("PCb", i, 0)] + ([("PCb", i, "h")] if half == 0 else [("PCb", i, 1)]), [psk(b)])
                consume(half, b)
                yield

        def gen_A(h):
            p = h % 2
            kb_, kd_, vb_, kbk, kdk, vbk = KBt[h % 3]
            SZ_ = SZs[h % 3]
            for role in range(3):
                widx = role * 8 + h

                def fill(pc, i, widx=widx):
                    w8_issue()
                    slot = w8_next(widx)
                    for half in range(2):
                        b = bankA()
                        if h == 0 and widx == 0:
                            for kc in range(KC):
                                MMG([(PS[b][:], W8[slot][:, kc * 128:(kc + 1) * 128], hTs(kc, half * 512, half * 512 + 512), kc == 0, kc == KC - 1)],
                                    [("w8", slot), ("hT", kc, half)], [psk(b)])
                        else:
                            MMG([(PS[b][:], W8[slot][:, kc * 128:(kc + 1) * 128], hTs(kc, half * 512, half * 512 + 512), kc == 0, kc == KC - 1) for kc in range(KC)],
                                [("w8", slot)] + [("hT", kc, half) for kc in range(KC)], [psk(b)])
                        CP("act", pc[:, 3 + half * 512: 3 + half * 512 + 512], PS[b][:], [psk(b)], [("PCb", i, half)])
                        yield

                def consume(half, b, role=role):
                    sl = slice(half * 512, half * 512 + 512)
                    sig_from_psum(b, half)
                    if role < 2:
                        TT("dve", SIL[:, sl], PS[b][:], SGA[:, sl], ALU.mult, [psk(b), ("SGA", half)], [("SILa", half)])
                    else:
                        TT("dve", ar(STMP, half * 512, half * 512 + 512), PS[b][:], SGA[:, sl], ALU.mult, [psk(b), ("SGA", half)], [ark(STMP)])
                yield from conv_pe(4, widx, gcw, widx * 4, fill, consume, bankA)
                if role < 2:
                    dst = (SQs, SKs)[role][p]
                    scale = float(128 ** -0.5) if role == 0 else 1.0
                    for half in range(2):
                        lo = half * 512
                        sl = slice(lo, lo + 512)
                        TT("pool", ar(STMP2, lo, lo + 512), SIL[:, sl], SIL[:, sl], ALU.mult, [("SILa", half)], [ark(STMP2)])
                        b = bankA()
                        MMG([(PS[b][:], onesK, ar(STMP2, lo, lo + 512), True, True)], [ark(STMP2), "CB"], [psk(b)])
                        ACT(SGA[:, sl], PS[b][:], AF.Ln, [psk(b), "EPSC"], [("SGA", half)], bias=EPSC[:, 0:1], scale=1.0)
                        ACT(SGA[:, sl], SGA[:, sl], AF.Exp, [("SGA", half)], [("SGA", half)], scale=-0.5)
                        STT("dve", ar(dst, lo, lo + 512), SIL[:, sl], scale, SGA[:, sl], ALU.mult, ALU.mult,
                            [("SILa", half), ("SGA", half)], [ark(dst)])
                        yield
            w8_issue()
            slot = w8_next(24 + h)
            for half in range(2):
                b = bankA()
                MMG([(PS[b][:], W8[slot][:, kc * 128:(kc + 1) * 128], hTs(kc, half * 512, half * 512 + 512), kc == 0, kc == KC - 1) for kc in range(KC)],
                    [("w8", slot)] + [("hT", kc, half) for kc in range(KC)], [psk(b)])
                sig_from_psum(b, half)
                TT("dve", ar(SZ_, half * 512, half * 512 + 512), PS[b][:], SGA[:, half * 512: half * 512 + 512], ALU.mult,
                   [psk(b), ("SGA", half)], [ark(SZ_)])
                yield
            for src, which in ((STMP, "v"), (SKs[p], "k")):
                b = bankA()
                psb = PS[b][:].bitcast(BF16)
                TRG([(psb[:, t * 128:(t + 1) * 128], ar(src, t * 128, (t + 1) * 128), identB) for t in range(NT)], [ark(src), "CB"], [psk(b)])
                ps3 = psb.rearrange("p (t d) -> p t d", d=128)
                if which == "v":
                    TT("dve", vb_.rearrange("p (t d) -> p t d", d=128), ps3, b3[:, :, h:h + 1].to_broadcast([128, NT, 128]), ALU.mult,
                       [psk(b), "beta"], [vbk])
                else:
                    TT("dve", kb_.rearrange("p (t d) -> p t d", d=128), ps3, beg3[:, :, h:h + 1].to_broadcast([128, NT, 128]), ALU.mult,
                       [psk(b), "beg"], [kbk])
                    TT("dve", kd_.rearrange("p (t d) -> p t d", d=128), ps3, EX4[:, :, 1, h:h + 1].to_broadcast([128, NT, 128]), ALU.mult,
                       [psk(b), "EX"], [kdk])
                yield

        def gen_B(h, job):
            p = h % 2
            B_, K_ = BS[job], BK[job]
            kb_, kd_, vb_, kbk, kdk, vbk = KBt[h % 3]
            QG_, QKT_, NWT_, Usb_, QGk, QKTk, NWTk, Usbk = RSt[p * 2 + job]
            SQ, SK = SQs[p], SKs[p]
            p0 = job * 4
            tk = lambda pi: slice((p0 + pi) * 128, (p0 + pi + 1) * 128)
            pl = lambda pi: slice(pi * 128, (pi + 1) * 128)
            GM_, SBm_, Dm_, DD_, Lb_, Ub_, QKb_, EGB_ = B_["GM"], B_["SBm"], B_["Dm"], B_["DD"], B_["Lb"], B_["Ub"], B_["QKb"], B_["EGB"]
            TT("pool", r4(GM_), bc4(M2), g3[:, p0:p0 + 4, h:h + 1].to_broadcast([128, 4, 128]), ALU.mult, ["graw", "CF"], [K_["GM"]])
            TT("pool", r4(EGB_), bc4(M2), g3[:, p0:p0 + 4, h:h + 1].to_broadcast([128, 4, 128]), ALU.mult, ["graw", "CF"], [K_["EGB"]])
            TT("pool", r4(SBm_), bc4(MS), b3[:, p0:p0 + 4, h:h + 1].to_broadcast([128, 4, 128]), ALU.mult, ["beta", "CF"], [K_["SBm"]])
            bE, bG = bankB(), bankB()
            MMG([(PS[bE][:, pl(pi)], GM_[:, pl(pi)], MS, True, True) for pi in range(4)], [K_["GM"], "CF"], [psk(bE)])
            MMG([(PS[bG][:, pl(pi)], onesK, EGB_[:, pl(pi)], True, True) for pi in range(4)], [K_["EGB"], "CB"], [psk(bG)])
            ACT(Dm_, PS[bE][:], AF.Exp, [psk(bE)], [K_["Dm"]])
            ACT(EGB_, PS[bG][:], AF.Exp, [psk(bG)], [K_["EGB"]])
            yield
            TT("pool", SBm_, Dm_, SBm_, ALU.mult, [K_["Dm"], K_["SBm"]], [K_["SBm"]])
            TT("pool", r4(DD_), r4(Dm_), bc4(M2T), ALU.mult, [K_["Dm"], "CF"], [K_["DD"]])
            bA, bQK = bankB(), bankB()
            MMG([(PS[bA][:, pl(pi)], ar(SK)[:, tk(pi)], ar(SK)[:, tk(pi)], True, True) for pi in range(4)], [ark(SK)], [psk(bA)])
            MMG([(PS[bQK][:, pl(pi)], ar(SQ)[:, tk(pi)], ar(SK)[:, tk(pi)], True, True) for pi in range(4)], [ark(SK), ark(SQ)], [psk(bQK)])
            TT("dve", Lb_, PS[bA][:], SBm_, ALU.mult, [psk(bA), K_["SBm"]], [K_["Lb"]])
            TT("dve", QKb_, PS[bQK][:], DD_, ALU.mult, [psk(bQK), K_["DD"]], [K_["QKb"]])
            TT("pool", QG_, ar(SQ, p0 * 128, p0 * 128 + 512), EGB_, ALU.mult, [ark(SQ), K_["EGB"]], [QGk])
            yield
            bU, bT = bankB(), bankB()
            pu = PS[bU][:].bitcast(BF16); pt = PS[bT][:].bitcast(BF16)
            TRG([(pu[:, pl(pi)], Lb_[:, pl(pi)], identB) for pi in range(4)], [K_["Lb"], "CB"], [psk(bU)])
            TRG([(pt[:, pl(pi)], QKb_[:, pl(pi)], identB) for pi in range(4)], [K_["QKb"], "CB"], [psk(bT)])
            CP("act", Ub_, pu[:, 0:512], [psk(bU)], [K_["Ub"]])
            CP("act", QKT_, pt[:, 0:512], [psk(bT)], [QKTk])
            TT("pool", r4(B_["P0"]), r4(Lb_), bc4(MLm[0]), ALU.mult, [K_["Lb"], "CB"], [K_["P0"]])
            TT("dve", r4(B_["Q0"]), bc4(identB), r4(B_["P0"]), ALU.subtract, [K_["P0"], "CB"], [K_["Q0"]])
            yield
            TT("pool", r4(B_["P1"]), r4(Ub_), bc4(MU1), ALU.mult, [K_["Ub"], "CB"], [K_["P1"]])
            TT("dve", r4(B_["X0"]), bc4(identB), r4(B_["P1"]), ALU.subtract, [K_["P1"], "CB"], [K_["X0"]])
            LOs = [("LOA", "dve"), ("LOB", "pool"), ("P1", "pool"), ("QKb", "pool"), ("Dm", "pool")]
            for li in range(5):
                nm, eng_ = LOs[li]
                TT(eng_, r4(B_[nm]), r4(Lb_), bc4(MLm[li + 1]), ALU.mult, [K_["Lb"], "CB"], [K_[nm]])
            yield
            Tdc, Tdk, Rc, Rk = B_["Q0"], K_["Q0"], B_["X0"], K_["X0"]
            M1, M1k = B_["P0"], K_["P0"]
            for li in range(5):
                LO, LOk = B_[LOs[li][0]], K_[LOs[li][0]]
                xn = "X%d" % ((li + 1) % 2); qn = "Q%d" % ((li + 1) % 2)
                Rn, Rnk, Tdn, Tdnk = B_[xn], K_[xn], B_[qn], K_[qn]
                bP = bankB()
                MMG([(PS[bP][:, pl(pi)], LO[:, pl(pi)], Rc[:, pl(pi)], True, True) for pi in range(4)], [LOk, Rk], [psk(bP)])
                CP("act", M1, PS[bP][:], [psk(bP)], [M1k])
                yield
                bX = bankB()
                MMG([(PS[bX][:, pl(pi)], Tdc[:, pl(pi)], M1[:, pl(pi)], True, True) for pi in range(4)], [Tdk, M1k], [psk(bX)])
                TT("dve", Rn, Rc, PS[bX][:], ALU.subtract, [psk(bX), Rk], [Rnk])
                yield
                if li < 4:
                    bQ = bankB()
                    pq = PS[bQ][:].bitcast(BF16)
                    TRG([(pq[:, pl(pi)], Rn[:, pl(pi)], identB) for pi in range(4)], [Rnk, "CB"], [psk(bQ)])
                    CP("dve", Tdn, pq[:, 0:512], [psk(bQ)], [Tdnk])
                    yield
                Tdc, Tdk, Rc, Rk = Tdn, Tdnk, Rn, Rnk
            TTm, TTk = Rc, Rk
            bUu, bW = bankB(), bankB()
            MMG([(PS[bW][:, pl(pi)], kb_[:, tk(pi)], TTm[:, pl(pi)], True, True) for pi in range(4)], [TTk, kbk], [psk(bW)])
            MMG([(PS[bUu][:, pl(pi)], TTm[:, pl(pi)], vb_[:, tk(pi)], True, True) for pi in range(4)], [TTk, vbk], [psk(bUu)])
            CP("act", Usb_, PS[bUu][:], [psk(bUu)], [Usbk])
            ACT(NWT_, PS[bW][:], AF.Identity, [psk(bW)], [NWTk], scale=-1.0)
            yield

        def gen_C(h, job):
            p = h % 2
            kb_, kd_, vb_, kbk, kdk, vbk = KBt[h % 3]
            QG_, QKT_, NWT_, Usb_, QGk, QKTk, NWTk, Usbk = RSt[p * 2 + job]
            jp = job
            p0 = job * 4
            hs = slice(h * 128, (h + 1) * 128)
            pl = lambda pi: slice(pi * 128, (pi + 1) * 128)
            bO = 6
            for pi in range(4):
                t = p0 + pi
                for ci in range(2):
                    r = slice(ci * 64, ci * 64 + 64)
                    cs = slice(pi * 128 + ci * 64, pi * 128 + ci * 64 + 64)
                    MMG([(PS[5][r, 0:128], NWT_[:, cs], Sbf[:, hs], True, True)], [NWTk, ("Sbf", h)], [psk(5)])
                    TT("dve", VN[r, :], PS[5][r, 0:128], Usb_[r, pl(pi)], ALU.add, [psk(5), Usbk], ["VN"])
                    yield
                    MMG([(PS[bO][:, cs], Sbf[:, hs], QG_[:, cs], True, False), (PS[bO][:, cs], VN[r, :], QKT_[r, cs], False, True)],
                        [("Sbf", h), QGk, "VN", QKTk], [psk(bO)])
                    MMG([(PS[7][:, 128:256], kd_[r, t * 128:(t + 1) * 128], VN[r, :], True, True)], [kdk, "VN"], [psk(7)])
                    STT("dve", Sbf[:, hs], Sst[:, hs], EX4[:, t, 2 + ci, h:h + 1], PS[7][:, 128:256], ALU.mult, ALU.add,
                        [psk(7), ("S", h), "EX"], [("Sbf", h)])
                    STT("dve", Sst[:, hs], Sst[:, hs], EX4[:, t, 2 + ci, h:h + 1], PS[7][:, 128:256], ALU.mult, ALU.add,
                        [psk(7), ("S", h), "EX"], [("S", h)])
                    yield
            CP("act", OT[:, p0 * 128:p0 * 128 + 512], PS[bO][:], [psk(bO)], [("OT", job)])
            yield

        def gen_onorm(h):
            p = h % 2
            for half in range(2):
                rms_rstd(lambda c, lo, hi: OT[:, lo:hi], lambda c, hf: [("OT", hf)], 1, onesV, half, ["TMPB"], bk=lambda: 6)
                yield
            STT("dve", TMP[:], OT[:], gnw[:, 0:1], RST[:], ALU.mult, ALU.mult,
                [("OT", 0), ("OT", 1), ("RST", 0), ("RST", 1), "gnw"], [("TMP", 0), ("TMP", 1)])
            TT("dve", ar(h), TMP[:], ar(SZs[h % 3]), ALU.mult, [("TMP", 0), ("TMP", 1), ark(SZs[h % 3])], [ark(h)])
            yield

        def interleave(*gens):
            gens = [g for g in gens if g is not None]
            while gens:
                for g in list(gens):
                    try:
                        next(g)
                        yield
                    except StopIteration:
                        gens.remove(g)

        def chain(*gens):
            for g in gens:
                yield from g

        def run(g):
            for _ in g:
                pass

        def gen_Cfull(h):
            yield from gen_C(h, 0)
            yield from gen_C(h, 1)
            yield from gen_onorm(h)

        nheads = 1 if stage == 4 else H
        run(gen_A(0))
        run(interleave(gen_A(1) if nheads > 1 else None, gen_B(0, 0), gen_B(0, 1)))
        for k in range(nheads):
            nb = k + 1 < nheads
            run(interleave(gen_A(k + 2) if k + 2 < nheads else None, gen_B(k + 1, 0) if nb else None, gen_B(k + 1, 1) if nb else None, gen_Cfull(k)))
        mD1 = S.mark()
        if os.environ.get("K_NORESCHED") != "1":
            S.reschedule(mD0, mD1)
        S.op("pool", lambda e: e.nop(), writes=coarse + list(BK[1].values()))
        if stage == 4:
            dump(OT[:], [("OT", 0), ("OT", 1)], 1024)
            CP("dve", TMP[:], ar(0), [ark(0)], [("TMP", 0), ("TMP", 1)])
            dump(TMP[:], [("TMP", 0), ("TMP", 1)], 1024)
            dump(SIL[:, 0:512], [("SILa", 0)], 512)
            break
        for _ in range(NSLOT22):
            w22_issue()
        for c in range(8):
            def evac_b(half, b):
                CP("act", SIL[:, half * 512: half * 512 + 512], PS[b][:], [psk(b)], [("SILa", half)])
            dense(32 + c, evac_b)

            def evac_c(half, b):
                CP("act", SGA[:, half * 512: half * 512 + 512], PS[b][:], [psk(b)], [("SGA", half)])
            dense(40 + c, evac_c)

            def fill(pc, i, c=c):
                def evac_x(half, b):
                    TT("dve", pc[:, 3 + half * 512: 3 + half * 512 + 512], PS[b][:], SGA[:, half * 512: half * 512 + 512], ALU.mult,
                       [psk(b), ("SGA", half)], [("PCb", i, half)])
                dense(48 + c, evac_x)
                return
                yield

            def consume(half, b, c=c):
                sl = slice(half * 512, half * 512 + 512)
                TT("dve", ar(SCs[c], half * 512, half * 512 + 512), PS[b][:], SIL[:, sl], ALU.mult, [psk(b), ("SILa", half)], [ark(SCs[c])])
            run(conv_pe(3, 24 + c, scw, c * 3, fill, consume, bank))
        for c in range(8):
            def evac_ya(half, b):
                CP("act", SIL[:, half * 512: half * 512 + 512], PS[b][:], [psk(b)], [("SILa", half)])
            dense_ar(72 + c, list(range(0, 8)), evac_ya)

            def evac_yb(half, b):
                CP("act", OT[:, half * 512: half * 512 + 512], PS[b][:], [psk(b)], [("OT", half)])
            dense_ar(80 + c, list(SCs), evac_yb)

            def evac_ga(half, b):
                sl = slice(half * 512, half * 512 + 512)
                sig_from_psum(b, half)
                TT("dve", SIL[:, sl], SIL[:, sl], SGA[:, sl], ALU.mult, [("SILa", half), ("SGA", half)], [("SILa", half)])
            dense(56 + c, evac_ga)

            def evac_gb(half, b, c=c):
                sl = slice(half * 512, half * 512 + 512)
                sig_from_psum(b, half)
                TT("dve", OT[:, sl], OT[:, sl], SGA[:, sl], ALU.mult, [("OT", half), ("SGA", half)], [("OT", half)])
                TT("dve", ar(MMs[c], half * 512, half * 512 + 512), SIL[:, sl], OT[:, sl], ALU.add, [("SILa", half), ("OT", half)], [ark(MMs[c])])
            dense(64 + c, evac_gb)
        for c in range(8):
            def evac_o(half, b, c=c):
                col = bsel * 8 + c
                sl = (half * 512, half * 512 + 512)
                STT("dve", xTs(c, *sl), PS[b][:], g1[:, col:col + 1], xTs(c, *sl), ALU.mult, ALU.add,
                    [psk(b), ("xT", c, half)] + MCK, [("xT", c, half)])
            dense_ar(88 + c, list(MMs), evac_o)
        if stage == 5:
            dump(xT[:, 0:1024], [("xT", 0, 0), ("xT", 0, 1)], 1024)
            break
        norm_mod(wm2, sh2, bsel, hTs, lambda c, hf: [("hT", c, hf)], [22, 23, 8, 9])
        split_first[0] = 4
        for j in range(NFF):
            def evac_g(half, b):
                sl = slice(half * 512, half * 512 + 512)
                sig_from_psum(b, half)
                TT("dve", SIL[:, sl], PS[b][:], SGA[:, sl], ALU.mult, [psk(b), ("SGA", half)], [("SILa", half)])
            dense(96 + j, evac_g)

            def evac_u(half, b, j=j):
                TT("dve", ar(j, half * 512, half * 512 + 512), PS[b][:], SIL[:, half * 512: half * 512 + 512], ALU.mult,
                   [psk(b), ("SILa", half)], [ark(j)])
            dense(96 + NFF + j, evac_u)
        for c in range(8):
            i22 = w22_used[0]; w22_used[0] += 1
            slot = i22 % NSLOT22
            for half in range(2):
                b = bank()
                MMG([(PS[b][:], W22[slot][:, kc * 128:(kc + 1) * 128], ar(kc, half * 512, half * 512 + 512), kc == 0, kc == NFF - 1) for kc in range(NFF)],
                    [("w22", slot)] + [ark(j) for j in range(NFF)], [psk(b)])
                col = bsel * 8 + c
                sl = (half * 512, half * 512 + 512)
                STT("dve", xTs(c, *sl), PS[b][:], g2[:, col:col + 1], xTs(c, *sl), ALU.mult, ALU.add,
                    [psk(b), ("xT", c, half)] + MCK, [("xT", c, half)])
            if c + NSLOT22 < 8:
                w22_issue()
        YT = AR[:, 0:8 * TB].bitcast(F32)
        for half in range(2):
            rms_rstd(xTs, lambda c, hf: [("xT", c, hf)], KC, onesD, half, [22, 23, 8, 9])
            lo = half * 512
            for c in range(KC):
                col = bsel * 8 + c
                tb_, tk_ = ((TMP, "TMP"), (SGA, "SGA"))[c % 2]
                STT("dve", tb_[:, lo:lo + 512], xTs(c, lo, lo + 512), wmf[:, col:col + 1], RST[:, lo:lo + 512], ALU.mult, ALU.mult,
                    [("xT", c, half), ("RST", half)] + MCK, [(tk_, half)])
                if c in (3, 7):
                    TS("dve", YT[:, c * 512:(c + 1) * 512], tb_[:, lo:lo + 512], shf[:, col:col + 1], ALU.add, [(tk_, half)] + MCK, [ark(c)])
                else:
                    ACT(YT[:, c * 512:(c + 1) * 512], tb_[:, lo:lo + 512], AF.Identity, [(tk_, half)] + MCK, [ark(c)],
                        bias=shf[:, col:col + 1], scale=1.0)
            for t4 in range(4):
                t = half * 4 + t4
                sg = t % 2
                for g in range(2):
                    b = bank()
                    TRG([(PS[b][:, q * 128:(q + 1) * 128], YT[:, (g * 4 + q) * 512 + t4 * 128: (g * 4 + q) * 512 + t4 * 128 + 128], identF) for q in range(4)],
                        [ark(g * 4 + q) for q in range(4)] + ["CF"], [psk(b)])
                    CP("act" if g == 0 else "dve", stage_x[sg][:, g * 512:(g + 1) * 512], PS[b][:], [psk(b)], [("stg", sg)])
                row0 = blk * TB + t * 128
                S.op("sp", (lambda dst, src: (lambda e: e.dma_start(out=dst, in_=src)))(out_d[row0:row0 + 128, :], stage_x[sg][:]),
                     reads=[("stg", sg)], dma=True, dmakey=("out", sg))

    if os.environ.get("K_SCHED", "ALL") == "ALL":
        S.reschedule(mALL0, S.mark())
    fin = [o for o in S.ops["sp"] if o.is_dma and (str(o.dmakey).startswith("dbg") or (isinstance(o.dmakey, tuple) and o.dmakey[0] == "out"))]
    last = S.op("sp", lambda e: e.nop())
    seen = set()
    for o in fin:
        if id(o) not in seen:
            seen.add(id(o)); last.deps.append(o)
    S.emit(st)
    st.close()
    return nc


def _prep_shared(inp):
    f = np.float32
    w_in = np.asarray(inp["w_in"], f)[0]
    def chunks(w, cols):
        out = np.empty((len(cols), 128, KC * 128), f)
        for i, c0 in enumerate(cols):
            out[i] = w[:, c0:c0 + 128].reshape(KC, 128, 128).transpose(1, 0, 2).reshape(128, KC * 128)
        return out
    cols_in = [i * 128 for i in range(32)] + [4112 + i * 128 for i in range(40)]
    w8 = np.concatenate([
        chunks(w_in, cols_in),
        chunks(np.asarray(inp["w_gdn_proj"], f)[0], [i * 128 for i in range(8)]),
        chunks(np.asarray(inp["w_sc_out"], f)[0], [i * 128 for i in range(8)]),
        chunks(np.asarray(inp["w_o"], f)[0], [i * 128 for i in range(8)]),
        chunks(np.asarray(inp["w_ffn_in"], f)[0], [i * 128 for i in range(44)]),
    ], axis=0)
    wfo = np.asarray(inp["w_ffn_out"], f)[0]
    w22 = np.empty((8, 128, NFF * 128), f)
    for c in range(8):
        w22[c] = wfo[:, c * 128:(c + 1) * 128].reshape(NFF, 128, 128).transpose(1, 0, 2).reshape(128, NFF * 128)
    wab = w_in[:, 4096:4112].reshape(KC, 128, 16).transpose(1, 0, 2).reshape(128, KC * 16)
    wada = chunks(np.asarray(inp["w_ada"], f)[0], [i * 128 for i in range(48)])
    wadaf = chunks(np.asarray(inp["w_ada_f"], f), [i * 128 for i in range(16)])
    bada = np.asarray(inp["b_ada"], f)[0].reshape(48, 128).T
    badaf = np.asarray(inp["b_ada_f"], f).reshape(16, 128).T
    nw = np.concatenate([np.asarray(inp[k], f).reshape(-1).reshape(8, 128).T for k in ("norm1_w", "norm2_w", "normf_w")], axis=1)
    gcw = np.asarray(inp["gdn_conv_w"], f)[0].reshape(4, 24, 128).transpose(2, 1, 0).reshape(128, 96)
    scw = np.asarray(inp["sc_conv_w"], f)[0].reshape(3, 8, 128).transpose(2, 1, 0).reshape(128, 24)
    hp = np.concatenate([np.broadcast_to(np.asarray(inp["gdn_a_log"], f).reshape(1, 8), (128, 8)),
                         np.broadcast_to(np.asarray(inp["gdn_dt_bias"], f).reshape(1, 8), (128, 8))], axis=1)
    gnw = np.asarray(inp["gdn_norm_w"], f).reshape(128, 1)
    cf, cbf = _masks()
    c_ = lambda a: np.ascontiguousarray(a, dtype=f)
    return dict(w8=c_(w8), w22=c_(w22), wab=c_(wab), wada=c_(wada), wadaf=c_(wadaf), bada=c_(bada), badaf=c_(badaf),
                nw=c_(nw), gcw=c_(gcw), scw=c_(scw), hp=c_(hp), gnw=c_(gnw), cf=cf, cbf=cbf)


def _in_maps(inp, ncores=NCORES):
    shared = _prep_shared(inp)
    x = np.asarray(inp["x"], np.float32)
    c = np.asarray(inp["c"], np.float32)
    maps = []
    for i in range(ncores):
        m = dict(shared)
        m["x"] = np.ascontiguousarray(x[2 * i:2 * i + 2].reshape(2 * S_LEN, D))
        cc = c[2 * i:2 * i + 2]
        m["cT"] = np.ascontiguousarray(cc.reshape(2, KC, 128).transpose(2, 1, 0).reshape(128, KC * 2))
        maps.append(m)
    return maps


def kernel(**inputs):
    nc = build_nc()
    maps = _in_maps(inputs)
    res = run_bass_kernel_spmd(nc, maps, core_ids=list(range(NCORES)))
    out = np.empty((16, S_LEN, D), np.float32)
    for i in range(NCORES):
        out[2 * i:2 * i + 2] = np.asarray(res.results[i]["out"], np.float32).reshape(2, S_LEN, D)
    return out
```
